# Optimizing a Trainium2 kernel written in Bass

```python
import jax
import jax.numpy as jnp
from jax import lax
import numpy as np

D_MODEL = 2048
BATCH = 1
SEQ = 16384
DEPTH = 2
DEC_BATCH = 16
DEC_SEQ = 2048
PAST_LEN = 128

N_MEM = 256
N_FOURIER_GROUPS = 4
FOURIER_GROUP_DIM = 256
FOURIER_DIM = N_FOURIER_GROUPS * FOURIER_GROUP_DIM
MLA_HEADS = 16
Q_LORA = 512
KV_LORA = 512
QK_NOPE = 128
QK_ROPE = 64
V_HEAD = 128
ROPE_THETA = 10000.0
Q_BLOCK = 128
MEM_HEADS = 4
MEM_HEAD_DIM = 256
MEM_DIM = MEM_HEADS * MEM_HEAD_DIM
N_BRANCH = 3
D_IN = FOURIER_DIM + Q_LORA + KV_LORA + QK_ROPE + MEM_DIM + N_BRANCH * D_MODEL
D_FF = 5632
CONV_WIDTH = 3
EPS = 1e-6

kernel_name = 'hybrid_fnet_mla_memory_encoder'


def _rmsnorm(x, g):
    xf = x.astype(jnp.float32)
    xf = xf * lax.rsqrt(jnp.mean(xf * xf, axis=-1, keepdims=True) + EPS)
    return (xf * g.astype(jnp.float32)).astype(x.dtype)


def _rope_tables(seq):
    inv_freq = 1.0 / (ROPE_THETA ** (jnp.arange(0, QK_ROPE, 2, dtype=jnp.float32) / QK_ROPE))
    ang = jnp.arange(seq, dtype=jnp.float32)[:, None] * inv_freq[None, :]
    return jnp.cos(ang), jnp.sin(ang)


def _rope(x, cos, sin):
    xf = x.astype(jnp.float32)
    x1, x2 = jnp.split(xf, 2, axis=-1)
    return jnp.concatenate([x1 * cos - x2 * sin, x2 * cos + x1 * sin], axis=-1).astype(x.dtype)


def _fourier(u):
    B, S, _ = u.shape
    ug = u.astype(jnp.float32).reshape(B, S, N_FOURIER_GROUPS, FOURIER_GROUP_DIM)
    yg = jnp.fft.fftn(ug, axes=(1, 3), norm='ortho').real
    return yg.reshape(B, S, FOURIER_DIM).astype(u.dtype)


def _mla(c_q, c_kv, cos, sin, q_norm, w_uq, kv_norm, w_ukv):
    B, S, _ = c_q.shape
    q = (_rmsnorm(c_q, q_norm) @ w_uq).reshape(B, S, MLA_HEADS, QK_NOPE + QK_ROPE)
    q_nope, q_pe = q[..., :QK_NOPE], q[..., QK_NOPE:]
    q_pe = _rope(q_pe, cos[:, None, :], sin[:, None, :])
    c_lat, k_pe = c_kv[..., :KV_LORA], c_kv[..., KV_LORA:]
    kv = (_rmsnorm(c_lat, kv_norm) @ w_ukv).reshape(B, S, MLA_HEADS, QK_NOPE + V_HEAD)
    k_nope, v = kv[..., :QK_NOPE], kv[..., QK_NOPE:]
    k_pe = _rope(k_pe, cos, sin)
    scale = (QK_NOPE + QK_ROPE) ** -0.5
    n_blk = S // Q_BLOCK
    qn_b = q_nope.reshape(B, n_blk, Q_BLOCK, MLA_HEADS, QK_NOPE).transpose(1, 0, 2, 3, 4)
    qr_b = q_pe.reshape(B, n_blk, Q_BLOCK, MLA_HEADS, QK_ROPE).transpose(1, 0, 2, 3, 4)

    def block(args):
        qn, qr = args
        s = (jnp.einsum('bqhd,bkhd->bhqk', qn, k_nope, preferred_element_type=jnp.float32)
             + jnp.einsum('bqhr,bkr->bhqk', qr, k_pe, preferred_element_type=jnp.float32)) * scale
        p = jax.nn.softmax(s, axis=-1).astype(v.dtype)
        return jnp.einsum('bhqk,bkhd->bqhd', p, v)

    o = lax.map(block, (qn_b, qr_b))
    return o.transpose(1, 0, 2, 3, 4).reshape(B, S, MLA_HEADS * V_HEAD)


def _mem_attn(q_mem, mem, mem_norm, w_mem_kv):
    B, S, _ = q_mem.shape
    kv = (_rmsnorm(mem, mem_norm) @ w_mem_kv).reshape(B, mem.shape[1], 2, MEM_HEADS, MEM_HEAD_DIM)
    k, v = kv[:, :, 0], kv[:, :, 1]
    q = q_mem.reshape(B, S, MEM_HEADS, MEM_HEAD_DIM)
    s = jnp.einsum('bqhd,bkhd->bhqk', q, k, preferred_element_type=jnp.float32) * (MEM_HEAD_DIM ** -0.5)
    p = jax.nn.softmax(s, axis=-1).astype(v.dtype)
    return jnp.einsum('bhqk,bkhd->bqhd', p, v).reshape(B, S, MEM_DIM)


def _mixer(h, mem, cos, sin, w_in, q_norm, w_uq, kv_norm, w_ukv, mem_norm, w_mem_kv,
           w_fourier_out, w_attn_out, w_mem_out, w_o):
    B, S, _ = h.shape
    z = h @ w_in
    s0 = FOURIER_DIM
    s1 = s0 + Q_LORA
    s2 = s1 + KV_LORA + QK_ROPE
    s3 = s2 + MEM_DIM
    f_in, c_q, c_kv, q_mem, gate_in = jnp.split(z, [s0, s1, s2, s3], axis=-1)
    y_f = _fourier(f_in) @ w_fourier_out
    y_a = _mla(c_q, c_kv, cos, sin, q_norm, w_uq, kv_norm, w_ukv) @ w_attn_out
    y_m = _mem_attn(q_mem, mem, mem_norm, w_mem_kv) @ w_mem_out
    g = jax.nn.sigmoid(gate_in.astype(jnp.float32)).astype(h.dtype).reshape(B, S, N_BRANCH, D_MODEL)
    merged = g[:, :, 0] * y_f + g[:, :, 1] * y_a + g[:, :, 2] * y_m
    return merged @ w_o


def _conv_ffn(h, w_gate, w_up, conv_w, conv_b, w_down):
    g = h @ w_gate
    gp = jnp.pad(g, ((0, 0), (1, 1), (0, 0)))
    g = gp[:, :-2] * conv_w[0] + gp[:, 1:-1] * conv_w[1] + gp[:, 2:] * conv_w[2] + conv_b
    return (jax.nn.gelu(g, approximate=True) * (h @ w_up)) @ w_down


def _trunk(x, mem, p):
    cos, sin = _rope_tables(x.shape[1])
    for l in range(DEPTH):
        h = _rmsnorm(x, p['pre_mix_norm'][l])
        y = _mixer(h, mem, cos, sin, p['w_in'][l], p['q_norm'][l], p['w_uq'][l],
                   p['kv_norm'][l], p['w_ukv'][l], p['mem_norm'][l], p['w_mem_kv'][l],
                   p['w_fourier_out'][l], p['w_attn_out'][l], p['w_mem_out'][l], p['w_o'][l])
        x = x + _rmsnorm(y, p['post_mix_norm'][l])
        h = _rmsnorm(x, p['pre_ffn_norm'][l])
        y = _conv_ffn(h, p['w_ffn_gate'][l], p['w_ffn_up'][l], p['ffn_conv_w'][l],
                      p['ffn_conv_b'][l], p['w_ffn_down'][l])
        x = x + _rmsnorm(y, p['post_ffn_norm'][l])
    return x


def _normal(k, shape, scale):
    return jax.random.normal(k, shape, jnp.float32) * scale


def _gain(k, shape):
    return 1.0 + 0.05 * jax.random.normal(k, shape, jnp.float32)


def setup_inputs(seed: int = 0) -> dict:
    key = jax.random.key(seed)
    ks = jax.random.split(key, 24)
    L = DEPTH
    return {
        'x_prompt': _normal(ks[0], (BATCH, SEQ, D_MODEL), 1.0),
        'x_sample': _normal(ks[1], (DEC_BATCH, DEC_SEQ, D_MODEL), 1.0),
        'mem_prompt': _normal(ks[2], (BATCH, N_MEM, D_MODEL), 1.0),
        'mem_sample': _normal(ks[3], (DEC_BATCH, N_MEM, D_MODEL), 1.0),
        'pre_mix_norm': _gain(ks[4], (L, D_MODEL)),
        'w_in': _normal(ks[5], (L, D_MODEL, D_IN), D_MODEL ** -0.5),
        'q_norm': _gain(ks[6], (L, Q_LORA)),
        'w_uq': _normal(ks[7], (L, Q_LORA, MLA_HEADS * (QK_NOPE + QK_ROPE)), Q_LORA ** -0.5),
        'kv_norm': _gain(ks[8], (L, KV_LORA)),
        'w_ukv': _normal(ks[9], (L, KV_LORA, MLA_HEADS * (QK_NOPE + V_HEAD)), KV_LORA ** -0.5),
        'mem_norm': _gain(ks[10], (L, D_MODEL)),
        'w_mem_kv': _normal(ks[11], (L, D_MODEL, 2 * MEM_DIM), D_MODEL ** -0.5),
        'w_fourier_out': _normal(ks[12], (L, FOURIER_DIM, D_MODEL), FOURIER_DIM ** -0.5),
        'w_attn_out': _normal(ks[13], (L, MLA_HEADS * V_HEAD, D_MODEL), (MLA_HEADS * V_HEAD) ** -0.5),
        'w_mem_out': _normal(ks[14], (L, MEM_DIM, D_MODEL), MEM_DIM ** -0.5),
        'w_o': _normal(ks[15], (L, D_MODEL, D_MODEL), D_MODEL ** -0.5),
        'post_mix_norm': _gain(ks[16], (L, D_MODEL)),
        'pre_ffn_norm': _gain(ks[17], (L, D_MODEL)),
        'w_ffn_gate': _normal(ks[18], (L, D_MODEL, D_FF), D_MODEL ** -0.5),
        'w_ffn_up': _normal(ks[19], (L, D_MODEL, D_FF), D_MODEL ** -0.5),
        'ffn_conv_w': _normal(ks[20], (L, CONV_WIDTH, D_FF), CONV_WIDTH ** -0.5),
        'ffn_conv_b': _normal(ks[21], (L, D_FF), 0.01),
        'w_ffn_down': _normal(ks[22], (L, D_FF, D_MODEL), D_FF ** -0.5),
        'post_ffn_norm': _gain(ks[23], (L, D_MODEL)),
    }


def reference(x_prompt, x_sample, mem_prompt, mem_sample, pre_mix_norm, w_in, q_norm, w_uq,
              kv_norm, w_ukv, mem_norm, w_mem_kv, w_fourier_out, w_attn_out, w_mem_out, w_o,
              post_mix_norm, pre_ffn_norm, w_ffn_gate, w_ffn_up, ffn_conv_w, ffn_conv_b,
              w_ffn_down, post_ffn_norm):
    params = dict(pre_mix_norm=pre_mix_norm, w_in=w_in, q_norm=q_norm, w_uq=w_uq,
                  kv_norm=kv_norm, w_ukv=w_ukv, mem_norm=mem_norm, w_mem_kv=w_mem_kv,
                  w_fourier_out=w_fourier_out, w_attn_out=w_attn_out, w_mem_out=w_mem_out,
                  w_o=w_o, post_mix_norm=post_mix_norm, pre_ffn_norm=pre_ffn_norm,
                  w_ffn_gate=w_ffn_gate, w_ffn_up=w_ffn_up, ffn_conv_w=ffn_conv_w,
                  ffn_conv_b=ffn_conv_b, w_ffn_down=w_ffn_down, post_ffn_norm=post_ffn_norm)
    y_prompt = _trunk(x_prompt, mem_prompt, params)
    y_sample = _trunk(x_sample, mem_sample, params)
    return (y_prompt, y_sample)
```

```python
import numpy as np
import ml_dtypes
from contextlib import ExitStack
import concourse.bass as bass
import concourse.mybir as mybir
from concourse.bass_utils import run_bass_kernel_spmd

F32 = mybir.dt.float32
BF16 = mybir.dt.bfloat16
AF = mybir.ActivationFunctionType
ALU = mybir.AluOpType
NPBF = ml_dtypes.bfloat16

D = 2048
KC = 16
T = 2048
L = 2
NH = 16
DFF = 5632
FC = 44
EPS = 1e-6
NCORES = 8
NPAR = 264
P_PREMIX, P_POSTMIX, P_PREFFN, P_POSTFFN, P_QN, P_KVN, P_MEMN, P_CW0, P_CW1, P_CW2, P_CB = 0, 16, 32, 48, 64, 68, 72, 88, 132, 176, 220
SND_ROWS = 1600
NSEG = 3


class Eng:
    def __init__(self, h, sem, name):
        self.h, self.sem, self.name, self.cnt, self.seen = h, sem, name, 0, {}


class Res:
    __slots__ = ("w", "r", "sem", "cnt", "excl")

    def __init__(self, sem=None, excl=False):
        self.w, self.r, self.sem, self.cnt, self.excl = None, [], sem, 0, excl


class Prog:
    def __init__(self, nc, es):
        self.nc, self.es = nc, es
        self.nsem = 0
        self.pe = Eng(nc.tensor, self.newsem(), "pe")
        self.act = Eng(nc.scalar, self.newsem(), "act")
        self.dve = Eng(nc.vector, self.newsem(), "dve")
        self.pool = Eng(nc.gpsimd, self.newsem(), "pool")
        self.sp = Eng(nc.sync, self.newsem(), "sp")
        self.engs = [self.pe, self.act, self.dve, self.pool, self.sp]
        self.dres = []
        self.free_res = []
        self.live_res = []

    def newsem(self):
        self.nsem += 1
        return self.es.enter_context(self.nc.semaphore(f"s{self.nsem}"))

    def res(self, dma=False, perm=False, excl=False):
        if not dma:
            return Res(None, excl)
        if not perm and self.free_res:
            r = self.free_res.pop()
            r.w, r.r = None, []
        else:
            r = Res(self.newsem())
            self.dres.append(r)
        if not perm:
            self.live_res.append(r)
        return r

    def wait(self, e, tok):
        if tok is None:
            return
        key, sem, val = tok
        if e.seen.get(key, 0) >= val:
            return
        e.h.wait_ge(sem, val)
        e.seen[key] = val

    def acq(self, e, reads, writes):
        for r in reads:
            self.wait(e, r.w)
            if r.excl:
                for t in r.r:
                    self.wait(e, t)
        for w in writes:
            self.wait(e, w.w)
            for t in w.r:
                self.wait(e, t)

    def rel(self, tok, reads, writes):
        for r in reads:
            if r.excl:
                r.w = tok
                r.r = []
                continue
            r.r = [t for t in r.r if t[0] != tok[0]]
            r.r.append(tok)
        for w in writes:
            w.w = tok
            w.r = []

    def mark(self, e, ins):
        ins.then_inc(e.sem, 1)
        e.cnt += 1
        return (id(e), e.sem, e.cnt)

    def op(self, e, fn, reads=(), writes=()):
        self.acq(e, reads, writes)
        tok = self.mark(e, fn())
        self.rel(tok, reads, writes)
        return tok

    def dma(self, q, out, in_, sres, reads=(), writes=()):
        self.acq(q, reads, writes)
        ins = q.h.dma_start(out=out, in_=in_)
        sres.cnt += 16
        ins.then_inc(sres.sem, 16)
        tok = (id(sres), sres.sem, sres.cnt)
        self.rel(tok, reads, writes)
        return tok

    def barrier(self):
        sp = self.sp
        for e in self.engs:
            if e is not sp and e.cnt:
                self.wait(sp, (id(e), e.sem, e.cnt))
        for r in self.dres:
            if r.cnt:
                self.wait(sp, (id(r), r.sem, r.cnt))
        sp.h.sem_inc(sp.sem, 1)
        sp.cnt += 1
        tok = (id(sp), sp.sem, sp.cnt)
        for e in self.engs:
            if e is not sp:
                self.wait(e, tok)
        for e in self.engs:
            for x in self.engs:
                e.seen[id(x)] = x.cnt
            for r in self.dres:
                e.seen[id(r)] = r.cnt
        self.free_res.extend(self.live_res)
        self.live_res = []


def build_program():
    nc = bass.Bass("TRN2", target_bir_lowering=False)
    es = ExitStack()
    P = Prog(nc, es)
    pe, act, dve, pool, sp = P.pe, P.act, P.dve, P.pool, P.sp

    def din(name, shape, dt=F32):
        return nc.dram_tensor(name, list(shape), dt, kind="ExternalInput").ap()

    def dscr(name, shape, dt=BF16):
        kind = "ExternalOutput" if name in CFG.get("dump", ()) else "Internal"
        return nc.dram_tensor(name, list(shape), dt, kind=kind).ap()

    xs = din("xs", [NSEG, T, D])
    mems = din("mems", [NSEG, 256, D])
    w_in_fm = din("w_in_fm", [L, 72, 128, KC, 128])
    w_in_kpe = din("w_in_kpe", [L, 128, KC, 64])
    w_uq_t = din("w_uq_t", [L, NH, 128, 4, 192])
    w_uk_t = din("w_uk_t", [L, NH, 128, 4, 128])
    w_uv_t = din("w_uv_t", [L, 128, 4, 2048])
    w_memk_t = din("w_memk_t", [L, 8, 128, KC, 128])
    w_memv_t = din("w_memv_t", [L, 8, 128, KC, 128])
    w_fo_t = din("w_fo_t", [L, 128, 8, 2048])
    w_mo_t = din("w_mo_t", [L, 16, 128, 8, 128])
    w_ao_t = din("w_ao_t", [L, 16, 128, KC, 128])
    w_o_t = din("w_o_t", [L, 16, 128, KC, 128])
    w_g_t = din("w_g_t", [L, FC, 128, KC, 128])
    w_u_t = din("w_u_t", [L, FC, 128, KC, 128])
    w_d_t = din("w_d_t", [L, 16, 4, 128, 11, 128])
    par_in = din("par_in", [128, L * NPAR])
    ident_in = din("ident_in", [128, 128])
    ones_in = din("ones_in", [128, 128], BF16)
    rt_in = din("rt_in", [64, 64], BF16)
    sel_in = din("sel_in", [128, 16])
    cc_in = din("cc_in", [128, 2, 2, 256], BF16)
    rope_in = din("rope_in", [2, 2, 64, T])
    dft_s = din("dft_s", [2, 4, 4, 128, 4, 512], BF16)
    dft_p = din("dft_p", [2, 4, 32 if 0 in CFG["segs"] else 1, 128, 4, 512], BF16)
    y_out = nc.dram_tensor("y_out", [NSEG, T, D], F32, kind="ExternalOutput").ap()
    XT = dscr("XT", [NSEG, KC, 128, T], F32)
    GATES = dscr("GATES", [48, 128, T])
    QMT = dscr("QMT", [8, 128, T])
    CQN = dscr("CQN", [4, 128, T])
    SND = dscr("SND", [SND_ROWS, 2048])
    RCV = dscr("RCV", [NCORES * SND_ROWS, 2048])
    QN = dscr("QN", [NH, 128, T])
    QR = dscr("QR", [NH, 64, T])
    KT = dscr("KT", [NH, 128, NCORES * T])
    VD = dscr("VD", [NH, 128, NCORES * 16, 128])
    PART = dscr("PART", [16, 128, T], F32)
    YT = dscr("YT", [16, 128, T])
    ACTT = dscr("ACTT", [FC, 128, T])
    WCS = dscr("WCS", [L, 16, 128, KC, 128])
    ESND = dscr("ESND", [128, 88], F32)
    ERCV = dscr("ERCV", [NCORES * 128, 88], F32)

    def sb(name, shape, dt):
        return es.enter_context(nc.sbuf_tensor(name, list(shape), dt))

    BIG = sb("BIG", [128, 65536], BF16)
    A1 = BIG[:, 0:32768].rearrange("p (k t) -> p k t", k=KC)
    A2 = BIG[:, 32768:65536].rearrange("p (k t) -> p k t", k=KC)
    WS = [sb(f"WS{i}", [128, 2048], BF16) for i in range(4)]
    WSr = [P.res(True, perm=True) for _ in range(4)]
    SB16 = [sb(f"SB16_{i}", [128, 2048], BF16) for i in range(3)]
    SB16r = [P.res(True, perm=True) for _ in range(3)]
    SF = [sb(f"SF{i}", [128, 2048], F32) for i in range(2)]
    SFr = [P.res(True, perm=True) for _ in range(2)]
    RSTD = sb("RSTD", [128, 2048], F32)
    RSTDr = P.res()
    TB = sb("TB", [128, 2048], F32)
    TBr = P.res(True, perm=True)
    PAR = sb("PAR", [128, L * NPAR], F32)
    IDENT = sb("IDENT", [128, 128], F32)
    ONES = sb("ONES", [128, 128], BF16)
    RT = sb("RT", [64, 64], BF16)
    SEL = sb("SEL", [128, 16], F32)
    CCS = sb("CCS", [128, 2, 2, 256], BF16)
    PS = [es.enter_context(nc.psum_tensor(f"ps{i}", [128, 512], F32)) for i in range(8)]
    PSr = [P.res(excl=True) for _ in range(8)]
    cres = P.res(True, perm=True)
    ccsem = P.newsem()
    cccnt = [0]

    for dst, src in ((PAR, par_in), (IDENT, ident_in), (ONES, ones_in), (RT, rt_in), (SEL, sel_in), (CCS, cc_in)):
        P.dma(sp, dst[:], src, cres)
    P.barrier()

    def par(l, off, m):
        c = l * NPAR + off + m
        return PAR[:, c:c + 1]

    wctr = [0]

    def wload(src_ap, kc, ncols, cast=True):
        s = wctr[0] % 4
        wctr[0] += 1
        view = WS[s][:, 0:kc * ncols].rearrange("p (k c) -> p k c", k=kc)
        P.dma(pool, view, src_ap, WSr[s], writes=[WSr[s]])
        return view, WSr[s]

    def mm_group(bank, bres, pairs, reads, start=True, stop=True):
        P.acq(pe, reads, [bres] if start else [])
        if not start:
            P.wait(pe, bres.w)
        n = len(pairs)
        ins = None
        for i, (l_, r_) in enumerate(pairs):
            ins = nc.tensor.matmul(bank, lhsT=l_, rhs=r_, start=(start and i == 0), stop=(stop and i == n - 1))
        tok = P.mark(pe, ins)
        P.rel(tok, reads, [bres])
        return tok

    evq = [0]

    def evac_copy(out_ap, bank_ap, bres, wres, eng=None):
        if eng is None:
            eng = act if (evq[0] % 2 == 0) else dve
            evq[0] += 1
        if eng is act:
            return P.op(act, lambda: nc.scalar.activation(out=out_ap, in_=bank_ap, func=AF.Copy), reads=[bres], writes=[wres])
        return P.op(dve, lambda: nc.vector.tensor_copy(out=out_ap, in_=bank_ap), reads=[bres], writes=[wres])

    def rstd_from_banks(banks, dim, out_ap_fn):
        for t, b in enumerate(banks):
            o = out_ap_fn(t)
            P.op(act, lambda: nc.scalar.activation(out=o, in_=PS[b][:], func=AF.Sqrt, scale=1.0 / dim, bias=EPS), reads=[PSr[b]], writes=[RSTDr])
            P.op(dve, lambda: nc.vector.reciprocal(out=o, in_=o), reads=[RSTDr], writes=[RSTDr])

    def transpose_in(src, ntok, dst, xt_dst, inbuf, bt, sqbufs, dstr=None):
        nblk = ntok // bt
        nsub = bt // 128
        inr = [P.res(True), P.res(True)]
        if dstr is None:
            dstr = P.res()
        for blk in range(nblk):
            ib = inbuf[blk % 2]
            ibv = ib.rearrange("p (s d) -> p s d", s=nsub)
            P.dma(sp, ibv, src[blk * bt:(blk + 1) * bt, :].rearrange("(s p) d -> p s d", p=128), inr[blk % 2], writes=[inr[blk % 2]])
            for m in range(KC):
                b = m % 4
                P.acq(pe, [inr[blk % 2]], [PSr[b]])
                ins = None
                for s_ in range(nsub):
                    ins = nc.tensor.transpose(PS[b][:, s_ * 128:(s_ + 1) * 128], ibv[:, s_, m * 128:(m + 1) * 128], IDENT[:])
                tok = P.mark(pe, ins)
                P.rel(tok, [inr[blk % 2]], [PSr[b]])
                bank = PS[b][:, 0:bt]
                if xt_dst is not None:
                    sfi = (blk * KC + m) % 8
                    sfv = SF[sfi // 4][:, (sfi % 4) * 512:(sfi % 4) * 512 + bt]
                    P.op(act, lambda: nc.scalar.activation(out=sfv, in_=bank, func=AF.Copy), reads=[PSr[b]], writes=[SFr[sfi // 4]])
                    P.dma(sp, xt_dst[m][:, blk * bt:(blk + 1) * bt], sfv, SFr[sfi // 4], reads=[SFr[sfi // 4]])
                P.op(dve, lambda: nc.vector.tensor_copy(out=dst[:, m, blk * bt:(blk + 1) * bt], in_=bank), reads=[PSr[b]], writes=[dstr])
                sqt, sqr_ = sqbufs[m % len(sqbufs)]
                sq = sqt[:, 0:bt]
                P.op(act, lambda: nc.scalar.activation(out=sq, in_=bank, func=AF.Square), reads=[PSr[b]], writes=[sqr_])
                mm_group(PS[4 + blk % 4][:, 0:bt], PSr[4 + blk % 4], [(ONES[:], sq)], [sqr_], start=(m == 0), stop=(m == KC - 1))
            if blk % 4 == 3 or blk == nblk - 1:
                nb = (blk % 4) + 1
                base = (blk // 4) * 4
                for j in range(nb):
                    o = RSTD[:, (base + j) * bt:(base + j) * bt + bt]
                    P.op(act, lambda: nc.scalar.activation(out=o, in_=PS[4 + j][:, 0:bt], func=AF.Sqrt, scale=1.0 / D, bias=EPS), reads=[PSr[4 + j]], writes=[RSTDr])
                    P.op(dve, lambda: nc.vector.reciprocal(out=o, in_=o), reads=[RSTDr], writes=[RSTDr])
        return dstr

    def apply_norm(dst, ntok, l, poff, dstr):
        for m in range(KC):
            P.op(dve, lambda: nc.vector.scalar_tensor_tensor(out=dst[:, m, 0:ntok], in0=dst[:, m, 0:ntok], scalar=par(l, poff, m), in1=RSTD[:, 0:ntok], op0=ALU.mult, op1=ALU.mult), reads=[RSTDr, dstr], writes=[dstr])

    def gemm_fm(A, kc, wsrc, nchunks, epilogue, ntok=T, ncols=128, banks=(0, 1, 2, 3), cast=True, pf=2, a_reads=()):
        nt = max(1, ntok // 512)
        bt = min(512, ntok)
        tiles = {}
        for n in range(min(pf, nchunks)):
            tiles[n] = wload(wsrc(n), kc, ncols, cast)
        g = 0
        for n in range(nchunks):
            if n + pf < nchunks:
                tiles[n + pf] = wload(wsrc(n + pf), kc, ncols, cast)
            wv, wr = tiles.pop(n)
            for t in range(nt):
                b = banks[g % len(banks)]
                g += 1
                tok = mm_group(PS[b][0:ncols, 0:bt], PSr[b], [(wv[:, k, :], A[:, k, t * 512:t * 512 + bt]) for k in range(kc)], [wr] + list(a_reads))
                epilogue(n, t, b, PS[b][0:ncols, 0:bt])

    def gen_wcs(l):
        wfo = A1[:, 0:8, :]
        r_wfo = r_sw
        P.dma(pool, wfo, w_fo_t[l], r_wfo, writes=[r_wfo])
        outb = A2
        r_out = P.res(True)
        g = 0
        for ci in range(2):
            for grp in range(4):
                for a in range(2):
                    kq = ci * 8 + grp * 2 + a
                    for nt in range(4):
                        b = g % 4
                        g += 1
                        mm_group(PS[b][:], PSr[b], [(CCS[:, ci, bb, a * 128:(a + 1) * 128], wfo[:, grp * 2 + bb, nt * 512:(nt + 1) * 512]) for bb in range(2)], [r_wfo])
                        evac_copy(outb[:, kq, nt * 512:(nt + 1) * 512], PS[b][:], PSr[b], r_out)
        for m in range(16):
            P.dma(sp, WCS[l][m], outb[:, :, m * 128:(m + 1) * 128], r_out, reads=[r_out])
        P.barrier()

    def op_w_in(l, seg):
        h = A1
        U_sb = BIG[:, 32768:32768 + 16384].rearrange("p (j c) -> p j c", j=16)
        r_u = P.res(True)
        tiles = {}
        for n in range(2):
            tiles[n] = wload(w_in_fm[l][n], KC, 128)
        g = 0
        for n in range(8):
            if n + 2 < 8:
                tiles[n + 2] = wload(w_in_fm[l][n + 2], KC, 128)
            wv, wr = tiles.pop(n)
            for jg in range(4):
                b = g % 4
                g += 1
                P.acq(pe, [wr], [PSr[b]])
                ins = None
                for jj in range(4):
                    j = jg * 4 + jj
                    for k in range(KC):
                        ins = nc.tensor.matmul(PS[b][:, jj * 128:(jj + 1) * 128], lhsT=h[:, k, j * 128:(j + 1) * 128], rhs=wv[:, k, :], start=(k == 0), stop=(k == KC - 1))
                tok = P.mark(pe, ins)
                P.rel(tok, [wr], [PSr[b]])
                evac_copy(U_sb[:, jg * 4:(jg + 1) * 4, n * 128:(n + 1) * 128], PS[b][:].rearrange("p (j c) -> p j c", j=4), PSr[b], r_u)
        udst = SND[576:1600, :].rearrange("r (h c) -> (r h) c", h=2).rearrange("(j p) c -> p j c", p=128)
        P.dma(sp, udst, U_sb, r_u, reads=[r_u])
        stg = [0]

        def ep_store(dst_fn, func=None):
            def ep(n, t, b, bank):
                i = stg[0] % 2
                o = SB16[i][:, t * 512:(t + 1) * 512]
                if func is None:
                    evac_copy(o, bank, PSr[b], SB16r[i])
                else:
                    P.op(act, lambda: nc.scalar.activation(out=o, in_=bank, func=func), reads=[PSr[b]], writes=[SB16r[i]])
                if t == 3:
                    P.dma(sp, dst_fn(n), SB16[i][:], SB16r[i], reads=[SB16r[i]])
                    stg[0] += 1
            return ep
        gemm_fm(h, KC, lambda n: w_in_fm[l][16 + n], 8, ep_store(lambda n: QMT[n]))
        gemm_fm(h, KC, lambda n: w_in_fm[l][24 + n], 48, ep_store(lambda n: GATES[n], AF.Sigmoid))
        P.barrier()
        LAT = BIG[:, 32768:32768 + 16384].rearrange("p (k t) -> p k t", k=8)
        KPE = BIG[0:64, 32768 + 16384:32768 + 16384 + 2048]
        r_lat = P.res()

        def ep_lat(n, t, b, bank):
            evac_copy(LAT[:, n, t * 512:(t + 1) * 512], bank, PSr[b], r_lat)
        gemm_fm(h, KC, lambda n: w_in_fm[l][8 + n], 8, ep_lat)

        def ep_kpe(n, t, b, bank):
            evac_copy(KPE[:, t * 512:(t + 1) * 512], bank, PSr[b], r_lat)
        gemm_fm(h, KC, lambda n: w_in_kpe[l], 1, ep_kpe, ncols=64)
        for which in range(2):
            for k4 in range(4):
                kk = which * 4 + k4
                i = k4 % 2
                P.op(dve, lambda: nc.vector.tensor_tensor(out=SB16[i][:], in0=LAT[:, kk, :], in1=LAT[:, kk, :], op=ALU.mult), reads=[r_lat], writes=[SB16r[i]])
                for t in range(4):
                    mm_group(PS[4 + t][:], PSr[4 + t], [(ONES[:], SB16[i][:, t * 512:(t + 1) * 512])], [SB16r[i]], start=(k4 == 0), stop=(k4 == 3))
            rstd_from_banks([4, 5, 6, 7], 512, lambda t: RSTD[:, t * 512:(t + 1) * 512])
            for k4 in range(4):
                kk = which * 4 + k4
                i = 2
                P.op(dve, lambda: nc.vector.scalar_tensor_tensor(out=SB16[i][:], in0=LAT[:, kk, :], scalar=par(l, P_QN if which == 0 else P_KVN, k4), in1=RSTD[:], op0=ALU.mult, op1=ALU.mult), reads=[r_lat, RSTDr], writes=[SB16r[i]])
                dst = CQN[k4] if which == 0 else SND[k4 * 128:(k4 + 1) * 128, :]
                P.dma(sp, dst, SB16[i][:], SB16r[i], reads=[SB16r[i]])
        rope_kpe_like(KPE, r_lat, 0 if seg == 0 else 1, lambda t: SND[512:576, t * 512:(t + 1) * 512], 64)
        P.barrier()

    ROPE_C = sb("ROPE_C", [64, T], BF16)
    ROPE_S = sb("ROPE_S", [64, T], BF16)
    r_rope = P.res(True, perm=True)
    r_sw = P.res(True, perm=True)
    rope_loaded = [None]

    def load_rope(st):
        if rope_loaded[0] == st:
            return
        rope_loaded[0] = st
        P.dma(pool, ROPE_C[:], rope_in[st][0], r_rope, writes=[r_rope])
        P.dma(pool, ROPE_S[:], rope_in[st][1], r_rope, writes=[r_rope])

    def rope_tile(src_ap, src_res, t, out_ap, out_res, bank_i):
        mm_group(PS[bank_i][0:64, :], PSr[bank_i], [(RT[:], src_ap)], [src_res])
        ta = TB[0:64, 0:512]
        tb = TB[0:64, 512:1024]
        P.op(dve, lambda: nc.vector.tensor_tensor(out=ta, in0=PS[bank_i][0:64, :], in1=ROPE_S[:, t * 512:(t + 1) * 512], op=ALU.mult), reads=[PSr[bank_i], r_rope], writes=[TBr])
        P.op(dve, lambda: nc.vector.tensor_tensor(out=tb, in0=src_ap, in1=ROPE_C[:, t * 512:(t + 1) * 512], op=ALU.mult), reads=[src_res, r_rope, TBr], writes=[TBr])
        P.op(dve, lambda: nc.vector.tensor_tensor(out=out_ap, in0=ta, in1=tb, op=ALU.add), reads=[TBr], writes=[out_res])

    def rope_kpe_like(src, src_res, st, dst_fn, npart):
        load_rope(st)
        for t in range(4):
            i = t % 2
            o = SB16[i][0:64, 0:512]
            rope_tile(src[:, t * 512:(t + 1) * 512], src_res, t, o, SB16r[i], 4 + t % 2)
            P.dma(sp, dst_fn(t), o, SB16r[i], reads=[SB16r[i]])

    def allgather(src, dst):
        P.barrier()
        ins = nc.gpsimd.collective_compute("AllGather", ALU.bypass, replica_groups=[list(range(NCORES))], ins=[src], outs=[dst])
        cccnt[0] += 1
        ins.then_inc(ccsem, 1)
        for e in P.engs:
            e.h.wait_ge(ccsem, cccnt[0])
        P.barrier()

    def op_fourier(seg):
        YF = A1
        r_yf = P.res()
        prompt = seg == 0
        TG = 32 if prompt else 4
        dft = dft_p if prompt else dft_s
        ubuf = [BIG[:, 32768 + i * 2048:32768 + (i + 1) * 2048].rearrange("p (a c) -> p a c", a=4) for i in range(3)]
        cbuf = [BIG[:, 32768 + 6144 + i * 2048:32768 + 6144 + (i + 1) * 2048].rearrange("p (a c) -> p a c", a=4) for i in range(3)]
        sbuf_ = [BIG[:, 32768 + 12288 + i * 2048:32768 + 12288 + (i + 1) * 2048].rearrange("p (a c) -> p a c", a=4) for i in range(3)]
        rb = [P.res(True) for _ in range(3)]

        def usrc(tg, chh):
            if prompt:
                r, j0 = tg // 4, (tg % 4) * 4
                base = RCV[r * SND_ROWS + 576:(r + 1) * SND_ROWS, :]
            else:
                j0 = tg * 4
                base = SND[576:1600, :]
            u = base.rearrange("r (h c) -> (r h) c", h=2)
            return u[j0 * 128:(j0 + 4) * 128, chh * 512:(chh + 1) * 512].rearrange("(a p) c -> p a c", p=128)

        def load(ks, chh, tg, i):
            P.dma(sp, ubuf[i], usrc(tg, chh), rb[i], writes=[rb[i]])
            P.dma(sp, cbuf[i], dft[0][ks][tg], rb[i], writes=[rb[i]])
            P.dma(sp, sbuf_[i], dft[1][ks][tg], rb[i], writes=[rb[i]])
        steps = [(ks, chh, tg) for ks in range(4) for chh in range(2) for tg in range(TG)]
        for i in range(min(2, len(steps))):
            load(*steps[i], i % 3)
        for si, (ks, chh, tg) in enumerate(steps):
            if si + 2 < len(steps):
                load(*steps[si + 2], (si + 2) % 3)
            i = si % 3
            first, last = tg == 0, tg == TG - 1
            P.acq(pe, [rb[i]], PSr if first else [])
            ins = None
            for c in range(4):
                for a in range(4):
                    nc.tensor.matmul(PS[c][:], lhsT=ubuf[i][:, a, c * 128:(c + 1) * 128], rhs=cbuf[i][:, a, :], start=(first and a == 0), stop=(last and a == 3))
                    ins = nc.tensor.matmul(PS[4 + c][:], lhsT=ubuf[i][:, a, c * 128:(c + 1) * 128], rhs=sbuf_[i][:, a, :], start=(first and a == 0), stop=(last and a == 3))
            tok = P.mark(pe, ins)
            P.rel(tok, [rb[i]], PSr if last else [])
            if last:
                for c in range(4):
                    evac_copy(YF[:, chh * 4 + c, ks * 512:(ks + 1) * 512], PS[c][:], PSr[c], r_yf)
                    evac_copy(YF[:, 8 + chh * 4 + c, ks * 512:(ks + 1) * 512], PS[4 + c][:], PSr[4 + c], r_yf)
        P.barrier()

    def op_memattn(l, seg):
        QM = A2[:, 0:8, :]
        OM = A2[:, 8:16, :]
        r_qm = P.res(True)
        r_om = P.res()
        P.dma(sp, QM, QMT.rearrange("k p t -> p k t"), r_qm, writes=[r_qm])
        MEMN = TB[:].bitcast(BF16).rearrange("p (k t) -> p k t", k=KC)
        KMEM = SB16[1][:].rearrange("p (k t) -> p k t", k=8)
        VMEM = SB16[2][:].rearrange("p (j c) -> p j c", j=2)
        dstr = transpose_in(mems[seg], 256, MEMN, None, [SF[0][:], SF[1][:]], 128, [(SB16[0], SB16r[0])], dstr=TBr)
        apply_norm(MEMN, 256, l, P_MEMN, dstr)
        r_kv = P.res()

        def ep_k(n, t, b, bank):
            evac_copy(KMEM[:, n, :], bank, PSr[b], r_kv)
        gemm_fm(MEMN, KC, lambda n: w_memk_t[l][n], 8, ep_k, ntok=256, a_reads=[dstr])
        tiles = {}
        for n in range(2):
            tiles[n] = wload(w_memv_t[l][n], KC, 128)
        for n in range(8):
            if n + 2 < 8:
                tiles[n + 2] = wload(w_memv_t[l][n + 2], KC, 128)
            wv, wr = tiles.pop(n)
            b = n % 4
            P.acq(pe, [wr, dstr], [PSr[b]])
            ins = None
            for j in range(2):
                for k in range(KC):
                    ins = nc.tensor.matmul(PS[b][:, j * 128:(j + 1) * 128], lhsT=MEMN[:, k, j * 128:(j + 1) * 128], rhs=wv[:, k, :], start=(k == 0), stop=(k == KC - 1))
            tok = P.mark(pe, ins)
            P.rel(tok, [wr, dstr], [PSr[b]])
            evac_copy(VMEM[:, :, n * 128:(n + 1) * 128], PS[b][:, 0:256].rearrange("p (j c) -> p j c", j=2), PSr[b], r_kv)
        scale = 256 ** -0.5
        pt = [SB16[0][:, 0:512], SB16[0][:, 512:1024], SB16[0][:, 1024:1536], SB16[0][:, 1536:2048]]
        ptr = [P.res() for _ in range(4)]
        g = 0
        for hh in range(4):
            for t in range(4):
                q = [QM[:, 2 * hh + dch, t * 512:(t + 1) * 512] for dch in range(2)]
                pis = []
                for jc in range(2):
                    b = 4 + g % 2
                    pi = g % 4
                    g += 1
                    mm_group(PS[b][:], PSr[b], [(KMEM[:, 2 * hh + dch, jc * 128:(jc + 1) * 128], q[dch]) for dch in range(2)], [r_kv, r_qm])
                    P.op(act, lambda: nc.scalar.activation(out=pt[pi], in_=PS[b][:], func=AF.Exp, scale=scale), reads=[PSr[b]], writes=[ptr[pi]])
                    pis.append(pi)
                mm_group(PS[6][:], PSr[6], [(ONES[:], pt[pi]) for pi in pis], [ptr[pi] for pi in pis])
                rec = RSTD[:, 0:512]
                P.op(dve, lambda: nc.vector.reciprocal(out=rec, in_=PS[6][:]), reads=[PSr[6]], writes=[RSTDr])
                for dv in range(2):
                    b = dv
                    mm_group(PS[b][:], PSr[b], [(VMEM[:, jc, (2 * hh + dv) * 128:(2 * hh + dv + 1) * 128], pt[pis[jc]]) for jc in range(2)], [r_kv] + [ptr[pi] for pi in pis])
                    P.op(dve, lambda: nc.vector.tensor_tensor(out=OM[:, 2 * hh + dv, t * 512:(t + 1) * 512], in0=PS[b][:], in1=rec, op=ALU.mult), reads=[PSr[b], RSTDr], writes=[r_om])
        P.barrier()


    def op_merge_a(l):
        YF = A1
        OM = A2[:, 8:16, :]
        gbuf = [BIG[:, 32768 + i * 2048:32768 + (i + 1) * 2048] for i in range(4)]
        gr = [P.res(True) for _ in range(4)]

        def loadg(m):
            i0 = (m % 2) * 2
            P.dma(sp, gbuf[i0], GATES[m], gr[i0], writes=[gr[i0]])
            P.dma(sp, gbuf[i0 + 1], GATES[32 + m], gr[i0 + 1], writes=[gr[i0 + 1]])
        tiles = {}
        tiles[0] = (wload(WCS[l][0], KC, 128, cast=False), wload(w_mo_t[l][0], 8, 128))
        loadg(0)
        for m in range(16):
            if m + 1 < 16:
                tiles[m + 1] = (wload(WCS[l][m + 1], KC, 128, cast=False), wload(w_mo_t[l][m + 1], 8, 128))
                loadg(m + 1)
            (wc, wcr), (wm, wmr) = tiles.pop(m)
            i0 = (m % 2) * 2
            sf = SF[m % 2]
            for t in range(4):
                bf, bm = t % 2, 2 + t % 2
                mm_group(PS[bf][:], PSr[bf], [(wc[:, k, :], YF[:, k, t * 512:(t + 1) * 512]) for k in range(KC)], [wcr])
                mm_group(PS[bm][:], PSr[bm], [(wm[:, k, :], OM[:, k, t * 512:(t + 1) * 512]) for k in range(8)], [wmr])
                t0 = TB[:, (t % 2) * 1024:(t % 2) * 1024 + 512]
                t1 = TB[:, (t % 2) * 1024 + 512:(t % 2) * 1024 + 1024]
                P.op(dve, lambda: nc.vector.tensor_tensor(out=t0, in0=PS[bf][:], in1=gbuf[i0][:, t * 512:(t + 1) * 512], op=ALU.mult), reads=[PSr[bf], gr[i0]], writes=[TBr])
                P.op(dve, lambda: nc.vector.tensor_tensor(out=t1, in0=PS[bm][:], in1=gbuf[i0 + 1][:, t * 512:(t + 1) * 512], op=ALU.mult), reads=[PSr[bm], gr[i0 + 1]], writes=[TBr])
                P.op(dve, lambda: nc.vector.tensor_tensor(out=sf[:, t * 512:(t + 1) * 512], in0=t0, in1=t1, op=ALU.add), reads=[TBr], writes=[SFr[m % 2]])
            P.dma(sp, PART[m], sf[:], SFr[m % 2], reads=[SFr[m % 2]])
        P.barrier()

    def op_mla(l, seg):
        prompt = seg == 0
        NR = NCORES if prompt else 1
        st = 0 if prompt else 1
        load_rope(st)
        cqn = A1[:, 0:4, :]
        r_cqn = P.res(True)
        P.dma(sp, cqn, CQN.rearrange("k p t -> p k t"), r_cqn, writes=[r_cqn])
        qraw = BIG[0:64, 8192:8192 + 2048]
        r_qraw = P.res()
        tiles = {}
        for n in range(2):
            tiles[n] = wload(w_uq_t[l][n], 4, 192)
        for hh in range(NH):
            if hh + 2 < NH:
                tiles[hh + 2] = wload(w_uq_t[l][hh + 2], 4, 192)
            wv, wr = tiles.pop(hh)
            i = hh % 2
            for t in range(4):
                b = t % 2
                mm_group(PS[b][:], PSr[b], [(wv[:, k, 0:128], cqn[:, k, t * 512:(t + 1) * 512]) for k in range(4)], [wr, r_cqn])
                evac_copy(SB16[i][:, t * 512:(t + 1) * 512], PS[b][:], PSr[b], SB16r[i])
                b2 = 2 + t % 2
                mm_group(PS[b2][0:64, :], PSr[b2], [(wv[:, k, 128:192], cqn[:, k, t * 512:(t + 1) * 512]) for k in range(4)], [wr, r_cqn])
                evac_copy(qraw[:, t * 512:(t + 1) * 512], PS[b2][0:64, :], PSr[b2], r_qraw, eng=act)
                rope_tile(qraw[:, t * 512:(t + 1) * 512], r_qraw, t, SB16[2][0:64, t * 512:(t + 1) * 512], SB16r[2], 4 + t % 2)
            P.dma(sp, QN[hh], SB16[i][:], SB16r[i], reads=[SB16r[i]])
            P.dma(sp, QR[hh], SB16[2][0:64, :], SB16r[2], reads=[SB16r[2]])
        P.barrier()
        ckv = [A1[:, 0:4, :], A1[:, 4:8, :]]
        r_ckv = [P.res(True), P.res(True)]
        WV = A1[:, 8:12, :]
        r_wv = r_sw
        VST = BIG[:, 12 * 2048:16 * 2048].rearrange("p (h j c) -> p h j c", h=4, j=16)
        r_vst = P.res(True)
        P.dma(pool, WV, w_uv_t[l], r_wv, writes=[r_wv])

        def lat_src(r):
            base = RCV[r * SND_ROWS:r * SND_ROWS + 512, :] if prompt else SND[0:512, :]
            return base.rearrange("(k p) t -> p k t", p=128)
        P.dma(sp, ckv[0], lat_src(0), r_ckv[0], writes=[r_ckv[0]])
        for r in range(NR):
            if r + 1 < NR:
                P.dma(sp, ckv[(r + 1) % 2], lat_src(r + 1), r_ckv[(r + 1) % 2], writes=[r_ckv[(r + 1) % 2]])
            cv, cr = ckv[r % 2], r_ckv[r % 2]
            stg = [0]

            def ep_k(n, t, b, bank):
                i = stg[0] % 2
                evac_copy(SB16[i][:, t * 512:(t + 1) * 512], bank, PSr[b], SB16r[i])
                if t == 3:
                    P.dma(sp, KT[n][:, r * T:(r + 1) * T], SB16[i][:], SB16r[i], reads=[SB16r[i]])
                    stg[0] += 1
            gemm_fm(cv, 4, lambda n: w_uk_t[l][n], NH, ep_k, a_reads=[cr])
            g = 0
            for hg in range(4):
                for j in range(16):
                    b = g % 4
                    g += 1
                    mm_group(PS[b][:], PSr[b], [(cv[:, k, j * 128:(j + 1) * 128], WV[:, k, hg * 512:(hg + 1) * 512]) for k in range(4)], [cr, r_wv])
                    evac_copy(VST[:, :, j, :], PS[b][:].rearrange("p (h c) -> p h c", h=4), PSr[b], r_vst)
                for hi in range(4):
                    P.dma(sp, VD[hg * 4 + hi][:, r * 16:(r + 1) * 16, :], VST[:, hi, :, :], r_vst, reads=[r_vst])
        P.barrier()
        O = A2
        r_o = P.res()
        TKV = NR * T
        kpe = BIG[0:64, 0:TKV]
        r_kpe = P.res(True)
        for r in range(NR):
            src = RCV[r * SND_ROWS + 512:r * SND_ROWS + 576, :] if prompt else SND[512:576, :]
            P.dma(sp, kpe[:, r * T:(r + 1) * T], src, r_kpe, writes=[r_kpe])
        base = 16384
        qn = [BIG[:, base + i * 2048:base + (i + 1) * 2048] for i in range(2)]
        qr = [BIG[0:64, base + 4096 + i * 2048:base + 4096 + (i + 1) * 2048] for i in range(2)]
        r_q = [P.res(True), P.res(True)]
        kb_ = [BIG[:, base + 8192 + i * 2048:base + 8192 + (i + 1) * 2048] for i in range(2)]
        vb_ = [BIG[:, base + 12288 + i * 2048:base + 12288 + (i + 1) * 2048].rearrange("p (c d) -> p c d", c=16) for i in range(2)]
        r_kv = [P.res(True), P.res(True)]
        pt = [SB16[0][:, 0:512], SB16[0][:, 512:1024], SB16[0][:, 1024:1536]]
        ptr = [P.res() for _ in range(3)]
        acc = [TB[:, i * 512:(i + 1) * 512] for i in range(4)]
        accr = [P.res() for _ in range(4)]
        accb = SB16[1][:, 0:512]
        scale = 192 ** -0.5
        blocks = [(hh, kb) for hh in range(NH) for kb in range(NR)]

        def loadq(hh):
            i = hh % 2
            P.dma(sp, qn[i], QN[hh], r_q[i], writes=[r_q[i]])
            P.dma(sp, qr[i], QR[hh], r_q[i], writes=[r_q[i]])

        def loadkv(bi):
            hh, kb = blocks[bi]
            i = bi % 2
            P.dma(sp, kb_[i], KT[hh][:, kb * T:(kb + 1) * T], r_kv[i], writes=[r_kv[i]])
            P.dma(sp, vb_[i], VD[hh][:, kb * 16:(kb + 1) * 16, :], r_kv[i], writes=[r_kv[i]])
        loadq(0)
        loadkv(0)
        tasks = []
        for bi, (hh, kb) in enumerate(blocks):
            for qt in range(4):
                for c in range(16):
                    tasks.append((bi, hh, kb, qt, c))
        NT_ = len(tasks)

        def emit_s(ti):
            bi, hh, kb, qt, c = tasks[ti]
            if qt == 0 and c == 1:
                if bi + 1 < len(blocks):
                    loadkv(bi + 1)
                if kb == 0 and hh + 1 < NH:
                    loadq(hh + 1)
            b = 4 + ti % 3
            mm_group(PS[b][:], PSr[b], [(kb_[bi % 2][:, c * 128:(c + 1) * 128], qn[hh % 2][:, qt * 512:(qt + 1) * 512]),
                                       (kpe[:, kb * T + c * 128:kb * T + (c + 1) * 128], qr[hh % 2][:, qt * 512:(qt + 1) * 512])],
                     [r_kv[bi % 2], r_q[hh % 2], r_kpe])
            pi = ti % 3
            P.op(act, lambda: nc.scalar.activation(out=pt[pi], in_=PS[b][:], func=AF.Exp, scale=scale), reads=[PSr[b]], writes=[ptr[pi]])

        emit_s(0)
        for ti, (bi, hh, kb, qt, c) in enumerate(tasks):
            if ti + 1 < NT_:
                emit_s(ti + 1)
            pi = ti % 3
            first = kb == 0 and c == 0
            last = kb == NR - 1 and c == 15
            mm_group(PS[qt][:], PSr[qt], [(vb_[bi % 2][:, c, :], pt[pi])], [r_kv[bi % 2], ptr[pi]], start=first, stop=last)
            if first:
                P.op(dve, lambda: nc.vector.tensor_copy(out=acc[qt], in_=pt[pi]), reads=[ptr[pi]], writes=[accr[qt]])
            else:
                P.op(dve, lambda: nc.vector.tensor_tensor(out=acc[qt], in0=acc[qt], in1=pt[pi], op=ALU.add), reads=[ptr[pi], accr[qt]], writes=[accr[qt]])
            if last:
                P.op(dve, lambda: nc.vector.tensor_copy(out=accb, in_=acc[qt]), reads=[accr[qt]], writes=[SB16r[1]])
                mm_group(PS[7][:], PSr[7], [(ONES[:], accb)], [SB16r[1]])
                rec = RSTD[:, 0:512]
                P.op(dve, lambda: nc.vector.reciprocal(out=rec, in_=PS[7][:]), reads=[PSr[7]], writes=[RSTDr])
                P.op(dve, lambda: nc.vector.tensor_tensor(out=O[:, hh, qt * 512:(qt + 1) * 512], in0=PS[qt][:], in1=rec, op=ALU.mult), reads=[PSr[qt], RSTDr], writes=[r_o])
        P.barrier()

    def op_merge_b(l):
        O = A2
        MG = A1
        r_mg = P.res()
        pbuf = SF
        gb = [SB16[0], SB16[1]]

        def loadm(m):
            i = m % 2
            P.dma(sp, gb[i][:], GATES[16 + m], SB16r[i], writes=[SB16r[i]])
            P.dma(sp, pbuf[i][:], PART[m], SFr[i], writes=[SFr[i]])
        tiles = {0: wload(w_ao_t[l][0], KC, 128)}
        loadm(0)
        for m in range(16):
            if m + 1 < 16:
                tiles[m + 1] = wload(w_ao_t[l][m + 1], KC, 128)
                loadm(m + 1)
            wv, wr = tiles.pop(m)
            i = m % 2
            for t in range(4):
                b = t % 4
                mm_group(PS[b][:], PSr[b], [(wv[:, k, :], O[:, k, t * 512:(t + 1) * 512]) for k in range(KC)], [wr])
                t0 = TB[:, (t % 2) * 512:(t % 2) * 512 + 512]
                P.op(dve, lambda: nc.vector.tensor_tensor(out=t0, in0=PS[b][:], in1=gb[i][:, t * 512:(t + 1) * 512], op=ALU.mult), reads=[PSr[b], SB16r[i]], writes=[TBr])
                P.op(dve, lambda: nc.vector.tensor_tensor(out=MG[:, m, t * 512:(t + 1) * 512], in0=t0, in1=pbuf[i][:, t * 512:(t + 1) * 512], op=ALU.add), reads=[TBr, SFr[i]], writes=[r_mg])
        P.barrier()

    def op_outproj(A, wtiles, part_in=None, part_out=None, a_reads=()):
        def loadm(m):
            wl = [(wload(ap, kc, 128), kc, ko) for (ap, kc, ko) in wtiles(m)]
            if part_in:
                P.dma(sp, SF[m % 2][:], PART[m], SFr[m % 2], writes=[SFr[m % 2]])
            return wl
        tl = {0: loadm(0)}
        for m in range(16):
            if m + 1 < 16:
                tl[m + 1] = loadm(m + 1)
            wl = tl.pop(m)
            i = m % 2
            for t in range(4):
                b = t
                pairs = []
                reads = list(a_reads)
                for (wv, wr), kc, ko in wl:
                    reads.append(wr)
                    pairs += [(wv[:, k, :], A[:, ko + k, t * 512:(t + 1) * 512]) for k in range(kc)]
                mm_group(PS[b][:], PSr[b], pairs, reads)
                if part_out:
                    evac_copy(SF[i][:, t * 512:(t + 1) * 512], PS[b][:], PSr[b], SFr[i])
                    continue
                src = PS[b][:]
                srcr = PSr[b]
                if part_in:
                    tmp = TB[:, (t % 2) * 512:(t % 2) * 512 + 512]
                    P.op(dve, lambda: nc.vector.tensor_tensor(out=tmp, in0=PS[b][:], in1=SF[i][:, t * 512:(t + 1) * 512], op=ALU.add), reads=[PSr[b], SFr[i]], writes=[TBr])
                    src, srcr = tmp, TBr
                ys = SB16[i][:, t * 512:(t + 1) * 512]
                P.op(act, lambda: nc.scalar.activation(out=ys, in_=src, func=AF.Copy), reads=[srcr], writes=[SB16r[i]])
                sq = SB16[2][:, (t % 2) * 512:(t % 2) * 512 + 512]
                P.op(act, lambda: nc.scalar.activation(out=sq, in_=src, func=AF.Square), reads=[srcr], writes=[SB16r[2]])
                mm_group(PS[4 + t][:], PSr[4 + t], [(ONES[:], sq)], [SB16r[2]], start=(m == 0), stop=(m == 15))
            if part_out:
                P.dma(sp, PART[m], SF[i][:], SFr[i], reads=[SFr[i]])
            else:
                P.dma(sp, YT[m], SB16[i][:], SB16r[i], reads=[SB16r[i]])
        if not part_out:
            rstd_from_banks([4, 5, 6, 7], D, lambda t: RSTD[:, t * 512:(t + 1) * 512])
        P.barrier()

    def op_resid(l, seg, post_off, next_l, next_off):
        XB = A1
        r_xb = P.res()
        ybuf = [BIG[:, 32768 + i * 2048:32768 + (i + 1) * 2048] for i in range(2)]
        yr = [P.res(True), P.res(True)]

        def load(m):
            i = m % 2
            P.dma(sp, ybuf[i], YT[m], yr[i], writes=[yr[i]])
            P.dma(sp, SF[i][:], XT[seg][m], SFr[i], writes=[SFr[i]])
        load(0)
        for m in range(16):
            if m + 1 < 16:
                load(m + 1)
            i = m % 2
            P.op(dve, lambda: nc.vector.scalar_tensor_tensor(out=TB[:], in0=ybuf[i], scalar=par(l, post_off, m), in1=RSTD[:], op0=ALU.mult, op1=ALU.mult), reads=[yr[i], RSTDr], writes=[TBr])
            P.op(dve, lambda: nc.vector.tensor_tensor(out=SF[i][:], in0=TB[:], in1=SF[i][:], op=ALU.add), reads=[TBr, SFr[i]], writes=[SFr[i]])
            if next_l is not None:
                P.op(act, lambda: nc.scalar.activation(out=XB[:, m, :], in_=SF[i][:], func=AF.Copy), reads=[SFr[i]], writes=[r_xb])
                sq = SB16[m % 2]
                P.op(act, lambda: nc.scalar.activation(out=sq[:], in_=SF[i][:], func=AF.Square), reads=[SFr[i]], writes=[SB16r[m % 2]])
                for t in range(4):
                    mm_group(PS[4 + t][:], PSr[4 + t], [(ONES[:], sq[:, t * 512:(t + 1) * 512])], [SB16r[m % 2]], start=(m == 0), stop=(m == 15))
            P.dma(sp, XT[seg][m], SF[i][:], SFr[i], reads=[SFr[i]])
        if next_l is not None:
            rstd_from_banks([4, 5, 6, 7], D, lambda t: RSTD[:, t * 512:(t + 1) * 512])
            apply_norm(XB, T, next_l, next_off, r_xb)
        P.barrier()


    F32V = BIG[:, 32768:65536].bitcast(F32)
    FG = F32V[:, 6144:6144 + 2050]
    r_fg = P.res()
    HALO = F32V[:, 8200:8288]
    r_halo = P.res()

    def op_ffn_edges(l):
        h = A1
        hedge = SB16[2][:, 0:32].rearrange("p (k e) -> p k e", k=KC)
        P.op(dve, lambda: nc.vector.tensor_copy(out=hedge[:, :, 0:1], in_=h[:, :, 0:1]), writes=[SB16r[2]])
        P.op(dve, lambda: nc.vector.tensor_copy(out=hedge[:, :, 1:2], in_=h[:, :, T - 1:T]), reads=[SB16r[2]], writes=[SB16r[2]])
        tiles = {}
        for n in range(2):
            tiles[n] = wload(w_g_t[l][n], KC, 128)
        for m in range(FC):
            if m + 2 < FC:
                tiles[m + 2] = wload(w_g_t[l][m + 2], KC, 128)
            wv, wr = tiles.pop(m)
            P.acq(pe, [wr, SB16r[2]], [PSr[0]] if m == 0 else [])
            ins = None
            for k in range(KC):
                ins = nc.tensor.matmul(PS[0][:, 2 * m:2 * m + 2], lhsT=wv[:, k, :], rhs=hedge[:, k, :], start=(k == 0), stop=(k == KC - 1))
            tok = P.mark(pe, ins)
            P.rel(tok, [wr, SB16r[2]], [PSr[0]] if m == FC - 1 else [])
        P.op(act, lambda: nc.scalar.activation(out=TB[:, 0:88], in_=PS[0][:, 0:88], func=AF.Copy), reads=[PSr[0]], writes=[TBr])
        P.dma(sp, ESND, TB[:, 0:88], TBr, reads=[TBr])
        allgather(ESND, ERCV)
        EB = SF[0][:, 0:8 * 88].rearrange("p (r c) -> p r c", r=8)
        P.dma(sp, EB, ERCV.rearrange("(r p) c -> p r c", p=128), SFr[0], writes=[SFr[0]])
        HL = HALO.rearrange("p (m e) -> p m e", e=2)
        P.op(dve, lambda: nc.vector.memset(HALO, 0.0), writes=[r_halo])
        for r in range(8):
            Er = EB[:, r, :].rearrange("p (m e) -> p m e", e=2)
            P.op(dve, lambda: nc.vector.scalar_tensor_tensor(out=HL[:, :, 0:1], in0=Er[:, :, 1:2], scalar=SEL[:, r:r + 1], in1=HL[:, :, 0:1], op0=ALU.mult, op1=ALU.add), reads=[SFr[0], r_halo], writes=[r_halo])
            P.op(dve, lambda: nc.vector.scalar_tensor_tensor(out=HL[:, :, 1:2], in0=Er[:, :, 0:1], scalar=SEL[:, 8 + r:9 + r], in1=HL[:, :, 1:2], op0=ALU.mult, op1=ALU.add), reads=[SFr[0], r_halo], writes=[r_halo])
        P.barrier()

    def op_ffn_main(l, seg):
        h = A1
        prompt = seg == 0
        C = F32V[:, 0:2048]
        Dd = F32V[:, 2048:4096]
        E = F32V[:, 4096:6144]
        rC, rD, rE = P.res(), P.res(), P.res()
        if not prompt:
            P.op(dve, lambda: nc.vector.memset(FG[:, 0:1], 0.0), writes=[r_fg])
            P.op(dve, lambda: nc.vector.memset(FG[:, 2049:2050], 0.0), writes=[r_fg])

        def loadw(m):
            return (wload(w_g_t[l][m], KC, 128), wload(w_u_t[l][m], KC, 128))
        tiles = {0: loadw(0)}
        for m in range(FC):
            if m + 1 < FC:
                tiles[m + 1] = loadw(m + 1)
            (wg, wgr), (wu, wur) = tiles.pop(m)
            i = m % 2
            for t in range(4):
                mm_group(PS[t][:], PSr[t], [(wg[:, k, :], h[:, k, t * 512:(t + 1) * 512]) for k in range(KC)], [wgr])
                P.op(act, lambda: nc.scalar.activation(out=FG[:, 1 + t * 512:1 + (t + 1) * 512], in_=PS[t][:], func=AF.Copy), reads=[PSr[t]], writes=[r_fg])
            if prompt:
                P.op(dve, lambda: nc.vector.tensor_copy(out=FG[:, 0:1], in_=HALO[:, 2 * m:2 * m + 1]), reads=[r_halo], writes=[r_fg])
                P.op(dve, lambda: nc.vector.tensor_copy(out=FG[:, 2049:2050], in_=HALO[:, 2 * m + 1:2 * m + 2]), reads=[r_halo], writes=[r_fg])
            for t in range(4):
                mm_group(PS[4 + t][:], PSr[4 + t], [(wu[:, k, :], h[:, k, t * 512:(t + 1) * 512]) for k in range(KC)], [wur])
            P.op(dve, lambda: nc.vector.tensor_scalar(out=C, in0=FG[:, 0:2048], scalar1=par(l, P_CW0, m), scalar2=par(l, P_CB, m), op0=ALU.mult, op1=ALU.add), reads=[r_fg], writes=[rC])
            P.op(dve, lambda: nc.vector.scalar_tensor_tensor(out=C, in0=FG[:, 1:2049], scalar=par(l, P_CW1, m), in1=C, op0=ALU.mult, op1=ALU.add), reads=[r_fg, rC], writes=[rC])
            P.op(dve, lambda: nc.vector.scalar_tensor_tensor(out=C, in0=FG[:, 2:2050], scalar=par(l, P_CW2, m), in1=C, op0=ALU.mult, op1=ALU.add), reads=[r_fg, rC], writes=[rC])
            P.op(act, lambda: nc.scalar.activation(out=Dd, in_=C, func=AF.Square), reads=[rC], writes=[rD])
            P.op(dve, lambda: nc.vector.tensor_scalar(out=Dd, in0=Dd, scalar1=0.044715, scalar2=1.0, op0=ALU.mult, op1=ALU.add), reads=[rD], writes=[rD])
            P.op(dve, lambda: nc.vector.tensor_tensor(out=Dd, in0=Dd, in1=C, op=ALU.mult), reads=[rD, rC], writes=[rD])
            P.op(act, lambda: nc.scalar.activation(out=E, in_=Dd, func=AF.Sigmoid, scale=1.5957691216057308), reads=[rD], writes=[rE])
            P.op(dve, lambda: nc.vector.tensor_tensor(out=Dd, in0=E, in1=C, op=ALU.mult), reads=[rE, rC, rD], writes=[rD])
            for t in range(4):
                P.op(dve, lambda: nc.vector.tensor_tensor(out=SB16[i][:, t * 512:(t + 1) * 512], in0=Dd[:, t * 512:(t + 1) * 512], in1=PS[4 + t][:], op=ALU.mult), reads=[rD, PSr[4 + t]], writes=[SB16r[i]])
            P.dma(sp, ACTT[m], SB16[i][:], SB16r[i], reads=[SB16r[i]])
        P.barrier()

    def op_ffn_down(l):
        AH = BIG[:, 0:22 * 2048].rearrange("p (k t) -> p k t", k=22)
        for half in range(2):
            r_a = P.res(True)
            for k in range(22):
                P.dma(sp, AH[:, k, :], ACTT[half * 22 + k], r_a, writes=[r_a] if k == 0 else [])
            r_a.w = (id(r_a), r_a.sem, r_a.cnt)
            op_outproj(AH, lambda m: [(w_d_t[l][m][half * 2], 11, 0), (w_d_t[l][m][half * 2 + 1], 11, 11)],
                       part_in=(half == 1), part_out=(half == 0), a_reads=[r_a])

    def op_transpose_out(seg):
        inb = [BIG[:, 32768 + i * 16384:32768 + (i + 1) * 16384].bitcast(F32).rearrange("p (m t) -> p m t", m=KC) for i in range(2)]
        inr = [P.res(True), P.res(True)]
        src = XT[seg].rearrange("m p t -> p m t")

        def load(blk):
            P.dma(sp, inb[blk % 2], src[:, :, blk * 512:(blk + 1) * 512], inr[blk % 2], writes=[inr[blk % 2]])
        load(0)
        g = 0
        for blk in range(4):
            if blk + 1 < 4:
                load(blk + 1)
            for sub in range(4):
                oi = (blk * 4 + sub) % 2
                for mg in range(4):
                    b = g % 8
                    g += 1
                    P.acq(pe, [inr[blk % 2]], [PSr[b]])
                    ins = None
                    for mm in range(4):
                        ins = nc.tensor.transpose(PS[b][:, mm * 128:(mm + 1) * 128], inb[blk % 2][:, mg * 4 + mm, sub * 128:(sub + 1) * 128], IDENT[:])
                    tok = P.mark(pe, ins)
                    P.rel(tok, [inr[blk % 2]], [PSr[b]])
                    evac_copy(SF[oi][:, mg * 512:(mg + 1) * 512], PS[b][:], PSr[b], SFr[oi])
                r0 = blk * 512 + sub * 128
                P.dma(sp, y_out[seg][r0:r0 + 128, :], SF[oi][:], SFr[oi], reads=[SFr[oi]])
        P.barrier()

    for l in range(L):
        gen_wcs(l)
    for seg in CFG["segs"]:
        prompt = seg == 0
        inbuf = [BIG[:, 32768 + i * 16384:32768 + (i + 1) * 16384].bitcast(F32) for i in range(2)]
        dstr = transpose_in(xs[seg], T, A1, XT[seg], inbuf, 512, [(SB16[0], SB16r[0]), (SB16[1], SB16r[1])])
        apply_norm(A1, T, 0, P_PREMIX, dstr)
        P.barrier()
        for l in range(CFG["layers"]):
            op_w_in(l, seg)
            if prompt:
                allgather(SND, RCV)
            op_fourier(seg)
            op_memattn(l, seg)
            op_merge_a(l)
            op_mla(l, seg)
            op_merge_b(l)
            op_outproj(A1, lambda m: [(w_o_t[l][m], KC, 0)])
            op_resid(l, seg, P_POSTMIX, l, P_PREFFN)
            if prompt:
                op_ffn_edges(l)
            op_ffn_main(l, seg)
            op_ffn_down(l)
            last = l == CFG["layers"] - 1
            op_resid(l, seg, P_POSTFFN, None if last else l + 1, P_PREMIX)
        op_transpose_out(seg)
    P.barrier()
    es.close()
    return nc


CFG = {"segs": [0, 1, 2], "layers": 2}


def _tile_fm(W):
    Lw, K, N = W.shape
    return np.ascontiguousarray(W.reshape(Lw, K // 128, 128, N // 128, 128).transpose(0, 3, 2, 1, 4))


def _fm_vec(v):
    Lw, n = v.shape
    return v.reshape(Lw, n // 128, 128).transpose(2, 0, 1)


_CONST_CACHE = {}


def _constants():
    if _CONST_CACHE:
        return _CONST_CACHE
    c = _CONST_CACHE
    c["ident_in"] = np.eye(128, dtype=np.float32)
    c["ones_in"] = np.ones((128, 128), dtype=NPBF)
    rt = np.zeros((64, 64), np.float32)
    for m in range(32):
        rt[m + 32, m] = -1.0
    for m in range(32, 64):
        rt[m - 32, m] = 1.0
    c["rt_in"] = rt.astype(NPBF)
    p = np.arange(128)[:, None, None]
    bb = np.arange(2)[None, :, None]
    m = np.arange(256)[None, None, :]
    ang = 2 * np.pi * (((bb * 128 + p) * m) % 256) / 256.0
    cc = np.stack([np.cos(ang), np.sin(ang)], axis=1) / 16.0
    c["cc_in"] = cc.astype(NPBF)
    inv_freq = 1.0 / (10000.0 ** (np.arange(0, 64, 2, dtype=np.float32) / 64.0))
    c["inv_freq"] = inv_freq.astype(np.float32)

    def dft_tiles(S, ntg, kbase):
        tab_c = np.cos(2 * np.pi * np.arange(S) / S) / np.sqrt(S)
        tab_s = -np.sin(2 * np.pi * np.arange(S) / S) / np.sqrt(S)
        t = (np.arange(ntg)[:, None, None] * 4 + np.arange(4)[None, None, :]) * 128 + np.arange(128)[None, :, None]
        out = np.empty((2, 4, ntg, 128, 4, 512), dtype=NPBF)
        for ks in range(4):
            k = kbase + ks * 512 + np.arange(512)
            ph = (t[..., None].astype(np.int64) * k[None, None, None, :].astype(np.int64)) % S
            out[0, ks] = tab_c[ph].astype(NPBF)
            out[1, ks] = tab_s[ph].astype(NPBF)
        return out
    c["dft_s"] = dft_tiles(2048, 4, 0)
    c["dft_p"] = [dft_tiles(16384, 32 if 0 in CFG["segs"] else 1, 2048 * cc_) for cc_ in range(NCORES)]
    return c


def _rope_table(base):
    c = _constants()
    pos = (base + np.arange(T)).astype(np.float32)
    ang = pos[:, None] * c["inv_freq"][None, :]
    cs = np.cos(ang).astype(np.float32).T
    sn = np.sin(ang).astype(np.float32).T
    return np.stack([np.concatenate([cs, cs], 0), np.concatenate([sn, sn], 0)], 0)


def _prep_shared(inp):
    w_in = inp["w_in"]
    sh = {}
    cols = np.concatenate([np.arange(0, 1024), np.arange(1024, 2048), np.arange(2112, 3136), np.arange(3136, 9280)])
    sh["w_in_fm"] = _tile_fm(w_in[:, :, cols])
    sh["w_in_kpe"] = np.ascontiguousarray(w_in[:, :, 2048:2112].reshape(L, KC, 128, 64).transpose(0, 2, 1, 3))
    sh["w_uq_t"] = np.ascontiguousarray(inp["w_uq"].reshape(L, 4, 128, NH, 192).transpose(0, 3, 2, 1, 4))
    ukv = inp["w_ukv"].reshape(L, 4, 128, NH, 256)
    sh["w_uk_t"] = np.ascontiguousarray(ukv[..., 0:128].transpose(0, 3, 2, 1, 4))
    sh["w_uv_t"] = np.ascontiguousarray(ukv[..., 128:256].transpose(0, 2, 1, 3, 4)).reshape(L, 128, 4, 2048)
    sh["w_memk_t"] = _tile_fm(inp["w_mem_kv"][:, :, 0:1024])
    sh["w_memv_t"] = _tile_fm(inp["w_mem_kv"][:, :, 1024:2048])
    sh["w_fo_t"] = np.ascontiguousarray(inp["w_fourier_out"].reshape(L, 8, 128, 2048).transpose(0, 2, 1, 3))
    sh["w_mo_t"] = _tile_fm(inp["w_mem_out"])
    sh["w_ao_t"] = _tile_fm(inp["w_attn_out"])
    sh["w_o_t"] = _tile_fm(inp["w_o"])
    sh["w_g_t"] = _tile_fm(inp["w_ffn_gate"])
    sh["w_u_t"] = _tile_fm(inp["w_ffn_up"])
    wd = _tile_fm(inp["w_ffn_down"])
    sh["w_d_t"] = np.ascontiguousarray(wd.reshape(L, 16, 128, 4, 11, 128).transpose(0, 1, 3, 2, 4, 5))
    par = np.zeros((128, L, NPAR), np.float32)
    for off, key in ((P_PREMIX, "pre_mix_norm"), (P_POSTMIX, "post_mix_norm"), (P_PREFFN, "pre_ffn_norm"), (P_POSTFFN, "post_ffn_norm"),
                     (P_QN, "q_norm"), (P_KVN, "kv_norm"), (P_MEMN, "mem_norm"), (P_CB, "ffn_conv_b")):
        v = _fm_vec(np.asarray(inp[key], np.float32))
        par[:, :, off:off + v.shape[2]] = v
    cw = np.asarray(inp["ffn_conv_w"], np.float32)
    for j, off in enumerate((P_CW0, P_CW1, P_CW2)):
        par[:, :, off:off + FC] = _fm_vec(np.ascontiguousarray(cw[:, j, :]))
    sh["par_in"] = np.ascontiguousarray(par.reshape(128, L * NPAR))
    c = _constants()
    for k in ("ident_in", "ones_in", "rt_in", "cc_in", "dft_s"):
        sh[k] = c[k]
    return sh


def make_in_maps(inp, cores=range(NCORES)):
    inp = {k: np.asarray(v) for k, v in inp.items()}
    sh = _prep_shared(inp)
    c = _constants()
    rope_s = _rope_table(0)
    in_maps = []
    for ci in cores:
        d = dict(sh)
        d["xs"] = np.ascontiguousarray(np.stack([inp["x_prompt"][0, ci * T:(ci + 1) * T], inp["x_sample"][2 * ci], inp["x_sample"][2 * ci + 1]], 0))
        d["mems"] = np.ascontiguousarray(np.stack([inp["mem_prompt"][0], inp["mem_sample"][2 * ci], inp["mem_sample"][2 * ci + 1]], 0))
        d["rope_in"] = np.ascontiguousarray(np.stack([_rope_table(ci * T), rope_s], 0)).astype(np.float32)
        sel = np.zeros((128, 16), np.float32)
        if ci - 1 >= 0:
            sel[:, ci - 1] = 1.0
        if ci + 1 < NCORES:
            sel[:, 8 + ci + 1] = 1.0
        d["sel_in"] = sel
        d["dft_p"] = c["dft_p"][ci] if 0 in CFG["segs"] else c["dft_p"][ci][:, :, 0:1]
        in_maps.append(d)
    return in_maps


def kernel(**inp):
    in_maps = make_in_maps(inp)
    nc = build_program()
    res = run_bass_kernel_spmd(nc, in_maps, core_ids=list(range(NCORES)))
    y_prompt = np.empty((1, NCORES * T, D), np.float32)
    y_sample = np.empty((2 * NCORES, T, D), np.float32)
    for ci in range(NCORES):
        yo = res.results[ci]["y_out"]
        y_prompt[0, ci * T:(ci + 1) * T] = yo[0]
        y_sample[2 * ci] = yo[1]
        y_sample[2 * ci + 1] = yo[2]
    kernel.last_results = res.results
    return (y_prompt, y_sample)
```

```python
import numpy as np
import ml_dtypes
from contextlib import ExitStack
import concourse.bass as bass
import concourse.mybir as mybir
from concourse.bass_utils import run_bass_kernel_spmd

F32 = mybir.dt.float32
BF16 = mybir.dt.bfloat16
AF = mybir.ActivationFunctionType
ALU = mybir.AluOpType
NPBF = ml_dtypes.bfloat16

D = 2048
KC = 16
T = 2048
L = 2
NH = 16
DFF = 5632
FC = 44
EPS = 1e-6
NCORES = 8
NPAR = 264
P_PREMIX, P_POSTMIX, P_PREFFN, P_POSTFFN, P_QN, P_KVN, P_MEMN, P_CW0, P_CW1, P_CW2, P_CB = 0, 16, 32, 48, 64, 68, 72, 88, 132, 176, 220
SND_ROWS = 1600
NSEG = 3


class Eng:
    def __init__(self, h, sem, name):
        self.h, self.sem, self.name, self.cnt, self.seen = h, sem, name, 0, {}


class Res:
    __slots__ = ("w", "r", "sem", "cnt", "excl")

    def __init__(self, sem=None, excl=False):
        self.w, self.r, self.sem, self.cnt, self.excl = None, [], sem, 0, excl


class Prog:
    def __init__(self, nc, es):
        self.nc, self.es = nc, es
        self.nsem = 0
        self.pe = Eng(nc.tensor, self.newsem(), "pe")
        self.act = Eng(nc.scalar, self.newsem(), "act")
        self.dve = Eng(nc.vector, self.newsem(), "dve")
        self.pool = Eng(nc.gpsimd, self.newsem(), "pool")
        self.sp = Eng(nc.sync, self.newsem(), "sp")
        self.engs = [self.pe, self.act, self.dve, self.pool, self.sp]
        self.dres = []
        self.free_res = []
        self.live_res = []

    def newsem(self):
        self.nsem += 1
        return self.es.enter_context(self.nc.semaphore(f"s{self.nsem}"))

    def res(self, dma=False, perm=False, excl=False):
        if not dma:
            return Res(None, excl)
        if not perm and self.free_res:
            r = self.free_res.pop()
            r.w, r.r = None, []
        else:
            r = Res(self.newsem())
            self.dres.append(r)
        if not perm:
            self.live_res.append(r)
        return r

    def wait(self, e, tok):
        if tok is None:
            return
        key, sem, val = tok
        if e.seen.get(key, 0) >= val:
            return
        e.h.wait_ge(sem, val)
        e.seen[key] = val

    def acq(self, e, reads, writes):
        for r in reads:
            self.wait(e, r.w)
            if r.excl:
                for t in r.r:
                    self.wait(e, t)
        for w in writes:
            self.wait(e, w.w)
            for t in w.r:
                self.wait(e, t)

    def rel(self, tok, reads, writes):
        for r in reads:
            if r.excl:
                r.w = tok
                r.r = []
                continue
            r.r = [t for t in r.r if t[0] != tok[0]]
            r.r.append(tok)
        for w in writes:
            w.w = tok
            w.r = []

    def mark(self, e, ins):
        ins.then_inc(e.sem, 1)
        e.cnt += 1
        return (id(e), e.sem, e.cnt)

    def op(self, e, fn, reads=(), writes=()):
        self.acq(e, reads, writes)
        tok = self.mark(e, fn())
        self.rel(tok, reads, writes)
        return tok

    def dma(self, q, out, in_, sres, reads=(), writes=()):
        self.acq(q, reads, writes)
        ins = q.h.dma_start(out=out, in_=in_)
        sres.cnt += 16
        ins.then_inc(sres.sem, 16)
        tok = (id(sres), sres.sem, sres.cnt)
        self.rel(tok, reads, writes)
        return tok

    def barrier(self):
        sp = self.sp
        for e in self.engs:
            if e is not sp and e.cnt:
                self.wait(sp, (id(e), e.sem, e.cnt))
        for r in self.dres:
            if r.cnt:
                self.wait(sp, (id(r), r.sem, r.cnt))
        sp.h.sem_inc(sp.sem, 1)
        sp.cnt += 1
        tok = (id(sp), sp.sem, sp.cnt)
        for e in self.engs:
            if e is not sp:
                self.wait(e, tok)
        for e in self.engs:
            for x in self.engs:
                e.seen[id(x)] = x.cnt
            for r in self.dres:
                e.seen[id(r)] = r.cnt
        self.free_res.extend(self.live_res)
        self.live_res = []


def build_program():
    nc = bass.Bass("TRN2", target_bir_lowering=False)
    es = ExitStack()
    P = Prog(nc, es)
    pe, act, dve, pool, sp = P.pe, P.act, P.dve, P.pool, P.sp

    def din(name, shape, dt=F32):
        return nc.dram_tensor(name, list(shape), dt, kind="ExternalInput").ap()

    def dscr(name, shape, dt=BF16):
        kind = "ExternalOutput" if name in CFG.get("dump", ()) else "Internal"
        return nc.dram_tensor(name, list(shape), dt, kind=kind).ap()

    xs = din("xs", [NSEG, T, D])
    mems = din("mems", [NSEG, 256, D])
    w_in_fm = din("w_in_fm", [L, 72, 128, KC, 128])
    w_in_kpe = din("w_in_kpe", [L, 128, KC, 64])
    w_uq_t = din("w_uq_t", [L, NH, 128, 4, 192])
    w_uk_t = din("w_uk_t", [L, NH, 128, 4, 128])
    w_uv_t = din("w_uv_t", [L, 128, 4, 2048])
    w_memk_t = din("w_memk_t", [L, 8, 128, KC, 128])
    w_memv_t = din("w_memv_t", [L, 8, 128, KC, 128])
    w_fo_t = din("w_fo_t", [L, 128, 8, 2048])
    w_mo_t = din("w_mo_t", [L, 16, 128, 8, 128])
    w_ao_t = din("w_ao_t", [L, 16, 128, KC, 128])
    w_o_t = din("w_o_t", [L, 16, 128, KC, 128])
    w_g_t = din("w_g_t", [L, FC, 128, KC, 128])
    w_u_t = din("w_u_t", [L, FC, 128, KC, 128])
    w_d_t = din("w_d_t", [L, 16, 4, 128, 11, 128])
    par_in = din("par_in", [128, L * NPAR])
    ident_in = din("ident_in", [128, 128])
    ones_in = din("ones_in", [128, 128], BF16)
    rt_in = din("rt_in", [64, 64], BF16)
    sel_in = din("sel_in", [128, 16])
    cc_in = din("cc_in", [128, 2, 2, 256], BF16)
    rope_in = din("rope_in", [2, 2, 64, T])
    dft_s = din("dft_s", [2, 4, 4, 128, 4, 512], BF16)
    dft_p = din("dft_p", [2, 4, 32 if 0 in CFG["segs"] else 1, 128, 4, 512], BF16)
    y_out = nc.dram_tensor("y_out", [NSEG, T, D], F32, kind="ExternalOutput").ap()
    XT = dscr("XT", [NSEG, KC, 128, T], F32)
    GATES = dscr("GATES", [48, 128, T])
    QMT = dscr("QMT", [8, 128, T])
    CQN = dscr("CQN", [4, 128, T])
    SND = dscr("SND", [SND_ROWS, 2048])
    RCV = dscr("RCV", [NCORES * SND_ROWS, 2048])
    QN = dscr("QN", [NH, 128, T])
    QR = dscr("QR", [NH, 64, T])
    KT = dscr("KT", [NH, 128, NCORES * T])
    VD = dscr("VD", [NH, 128, NCORES * 16, 128])
    PART = dscr("PART", [16, 128, T], F32)
    YT = dscr("YT", [16, 128, T])
    ACTT = dscr("ACTT", [FC, 128, T])
    WCS = dscr("WCS", [L, 16, 128, KC, 128])
    ESND = dscr("ESND", [128, 88], F32)
    ERCV = dscr("ERCV", [NCORES * 128, 88], F32)

    def sb(name, shape, dt):
        return es.enter_context(nc.sbuf_tensor(name, list(shape), dt))

    BIG = sb("BIG", [128, 65536], BF16)
    A1 = BIG[:, 0:32768].rearrange("p (k t) -> p k t", k=KC)
    A2 = BIG[:, 32768:65536].rearrange("p (k t) -> p k t", k=KC)
    WS = [sb(f"WS{i}", [128, 2048], BF16) for i in range(4)]
    WSr = [P.res(True, perm=True) for _ in range(4)]
    SB16 = [sb(f"SB16_{i}", [128, 2048], BF16) for i in range(3)]
    SB16r = [P.res(True, perm=True) for _ in range(3)]
    SF = [sb(f"SF{i}", [128, 2048], F32) for i in range(2)]
    SFr = [P.res(True, perm=True) for _ in range(2)]
    RSTD = sb("RSTD", [128, 2048], F32)
    RSTDr = P.res()
    TB = sb("TB", [128, 2048], F32)
    TBr = P.res(True, perm=True)
    PAR = sb("PAR", [128, L * NPAR], F32)
    IDENT = sb("IDENT", [128, 128], F32)
    ONES = sb("ONES", [128, 128], BF16)
    RT = sb("RT", [64, 64], BF16)
    SEL = sb("SEL", [128, 16], F32)
    CCS = sb("CCS", [128, 2, 2, 256], BF16)
    PS = [es.enter_context(nc.psum_tensor(f"ps{i}", [128, 512], F32)) for i in range(8)]
    PSr = [P.res(excl=True) for _ in range(8)]
    cres = P.res(True, perm=True)
    ccsem = P.newsem()
    cccnt = [0]

    for dst, src in ((PAR, par_in), (IDENT, ident_in), (ONES, ones_in), (RT, rt_in), (SEL, sel_in), (CCS, cc_in)):
        P.dma(sp, dst[:], src, cres)
    P.barrier()

    def par(l, off, m):
        c = l * NPAR + off + m
        return PAR[:, c:c + 1]

    wctr = [0]

    def wload(src_ap, kc, ncols, cast=True):
        s = wctr[0] % 4
        wctr[0] += 1
        view = WS[s][:, 0:kc * ncols].rearrange("p (k c) -> p k c", k=kc)
        P.dma(pool, view, src_ap, WSr[s], writes=[WSr[s]])
        return view, WSr[s]

    def mm_group(bank, bres, pairs, reads, start=True, stop=True):
        P.acq(pe, reads, [bres] if start else [])
        if not start:
            P.wait(pe, bres.w)
        n = len(pairs)
        ins = None
        for i, (l_, r_) in enumerate(pairs):
            ins = nc.tensor.matmul(bank, lhsT=l_, rhs=r_, start=(start and i == 0), stop=(stop and i == n - 1))
        tok = P.mark(pe, ins)
        P.rel(tok, reads, [bres])
        return tok

    evq = [0]

    def evac_copy(out_ap, bank_ap, bres, wres, eng=None):
        if eng is None:
            eng = act if (evq[0] % 2 == 0) else dve
            evq[0] += 1
        if eng is act:
            return P.op(act, lambda: nc.scalar.activation(out=out_ap, in_=bank_ap, func=AF.Copy), reads=[bres], writes=[wres])
        return P.op(dve, lambda: nc.vector.tensor_copy(out=out_ap, in_=bank_ap), reads=[bres], writes=[wres])

    def rstd_from_banks(banks, dim, out_ap_fn):
        for t, b in enumerate(banks):
            o = out_ap_fn(t)
            P.op(act, lambda: nc.scalar.activation(out=o, in_=PS[b][:], func=AF.Sqrt, scale=1.0 / dim, bias=EPS), reads=[PSr[b]], writes=[RSTDr])
            P.op(dve, lambda: nc.vector.reciprocal(out=o, in_=o), reads=[RSTDr], writes=[RSTDr])

    def transpose_in(src, ntok, dst, xt_dst, inbuf, bt, sqbufs, dstr=None):
        nblk = ntok // bt
        nsub = bt // 128
        inr = [P.res(True), P.res(True)]
        if dstr is None:
            dstr = P.res()
        for blk in range(nblk):
            ib = inbuf[blk % 2]
            ibv = ib.rearrange("p (s d) -> p s d", s=nsub)
            P.dma(sp, ibv, src[blk * bt:(blk + 1) * bt, :].rearrange("(s p) d -> p s d", p=128), inr[blk % 2], writes=[inr[blk % 2]])
            for m in range(KC):
                b = m % 4
                P.acq(pe, [inr[blk % 2]], [PSr[b]])
                ins = None
                for s_ in range(nsub):
                    ins = nc.tensor.transpose(PS[b][:, s_ * 128:(s_ + 1) * 128], ibv[:, s_, m * 128:(m + 1) * 128], IDENT[:])
                tok = P.mark(pe, ins)
                P.rel(tok, [inr[blk % 2]], [PSr[b]])
                bank = PS[b][:, 0:bt]
                if xt_dst is not None:
                    sfi = (blk * KC + m) % 8
                    sfv = SF[sfi // 4][:, (sfi % 4) * 512:(sfi % 4) * 512 + bt]
                    P.op(act, lambda: nc.scalar.activation(out=sfv, in_=bank, func=AF.Copy), reads=[PSr[b]], writes=[SFr[sfi // 4]])
                    P.dma(sp, xt_dst[m][:, blk * bt:(blk + 1) * bt], sfv, SFr[sfi // 4], reads=[SFr[sfi // 4]])
                P.op(dve, lambda: nc.vector.tensor_copy(out=dst[:, m, blk * bt:(blk + 1) * bt], in_=bank), reads=[PSr[b]], writes=[dstr])
                sqt, sqr_ = sqbufs[m % len(sqbufs)]
                sq = sqt[:, 0:bt]
                P.op(act, lambda: nc.scalar.activation(out=sq, in_=bank, func=AF.Square), reads=[PSr[b]], writes=[sqr_])
                mm_group(PS[4 + blk % 4][:, 0:bt], PSr[4 + blk % 4], [(ONES[:], sq)], [sqr_], start=(m == 0), stop=(m == KC - 1))
            if blk % 4 == 3 or blk == nblk - 1:
                nb = (blk % 4) + 1
                base = (blk // 4) * 4
                for j in range(nb):
                    o = RSTD[:, (base + j) * bt:(base + j) * bt + bt]
                    P.op(act, lambda: nc.scalar.activation(out=o, in_=PS[4 + j][:, 0:bt], func=AF.Sqrt, scale=1.0 / D, bias=EPS), reads=[PSr[4 + j]], writes=[RSTDr])
                    P.op(dve, lambda: nc.vector.reciprocal(out=o, in_=o), reads=[RSTDr], writes=[RSTDr])
        return dstr

    def apply_norm(dst, ntok, l, poff, dstr):
        for m in range(KC):
            P.op(dve, lambda: nc.vector.scalar_tensor_tensor(out=dst[:, m, 0:ntok], in0=dst[:, m, 0:ntok], scalar=par(l, poff, m), in1=RSTD[:, 0:ntok], op0=ALU.mult, op1=ALU.mult), reads=[RSTDr, dstr], writes=[dstr])

    def gemm_fm(A, kc, wsrc, nchunks, epilogue, ntok=T, ncols=128, banks=(0, 1, 2, 3), cast=True, pf=2, a_reads=()):
        nt = max(1, ntok // 512)
        bt = min(512, ntok)
        tiles = {}
        for n in range(min(pf, nchunks)):
            tiles[n] = wload(wsrc(n), kc, ncols, cast)
        g = 0
        for n in range(nchunks):
            if n + pf < nchunks:
                tiles[n + pf] = wload(wsrc(n + pf), kc, ncols, cast)
            wv, wr = tiles.pop(n)
            for t in range(nt):
                b = banks[g % len(banks)]
                g += 1
                tok = mm_group(PS[b][0:ncols, 0:bt], PSr[b], [(wv[:, k, :], A[:, k, t * 512:t * 512 + bt]) for k in range(kc)], [wr] + list(a_reads))
                epilogue(n, t, b, PS[b][0:ncols, 0:bt])

    def gen_wcs(l):
        wfo = A1[:, 0:8, :]
        r_wfo = r_sw
        P.dma(pool, wfo, w_fo_t[l], r_wfo, writes=[r_wfo])
        outb = A2
        r_out = P.res(True)
        g = 0
        for ci in range(2):
            for grp in range(4):
                for a in range(2):
                    kq = ci * 8 + grp * 2 + a
                    for nt in range(4):
                        b = g % 4
                        g += 1
                        mm_group(PS[b][:], PSr[b], [(CCS[:, ci, bb, a * 128:(a + 1) * 128], wfo[:, grp * 2 + bb, nt * 512:(nt + 1) * 512]) for bb in range(2)], [r_wfo])
                        evac_copy(outb[:, kq, nt * 512:(nt + 1) * 512], PS[b][:], PSr[b], r_out)
        for m in range(16):
            P.dma(sp, WCS[l][m], outb[:, :, m * 128:(m + 1) * 128], r_out, reads=[r_out])
        P.barrier()

    def op_w_in(l, seg):
        h = A1
        U_sb = BIG[:, 32768:32768 + 16384].rearrange("p (j c) -> p j c", j=16)
        r_u = P.res(True)
        tiles = {}
        for n in range(2):
            tiles[n] = wload(w_in_fm[l][n], KC, 128)
        g = 0
        for n in range(8):
            if n + 2 < 8:
                tiles[n + 2] = wload(w_in_fm[l][n + 2], KC, 128)
            wv, wr = tiles.pop(n)
            for jg in range(4):
                b = g % 4
                g += 1
                P.acq(pe, [wr], [PSr[b]])
                ins = None
                for jj in range(4):
                    j = jg * 4 + jj
                    for k in range(KC):
                        ins = nc.tensor.matmul(PS[b][:, jj * 128:(jj + 1) * 128], lhsT=h[:, k, j * 128:(j + 1) * 128], rhs=wv[:, k, :], start=(k == 0), stop=(k == KC - 1))
                tok = P.mark(pe, ins)
                P.rel(tok, [wr], [PSr[b]])
                evac_copy(U_sb[:, jg * 4:(jg + 1) * 4, n * 128:(n + 1) * 128], PS[b][:].rearrange("p (j c) -> p j c", j=4), PSr[b], r_u)
        udst = SND[576:1600, :].rearrange("r (h c) -> (r h) c", h=2).rearrange("(j p) c -> p j c", p=128)
        P.dma(sp, udst, U_sb, r_u, reads=[r_u])
        stg = [0]

        def ep_store(dst_fn, func=None):
            def ep(n, t, b, bank):
                i = stg[0] % 2
                o = SB16[i][:, t * 512:(t + 1) * 512]
                if func is None:
                    evac_copy(o, bank, PSr[b], SB16r[i])
                else:
                    P.op(act, lambda: nc.scalar.activation(out=o, in_=bank, func=func), reads=[PSr[b]], writes=[SB16r[i]])
                if t == 3:
                    P.dma(sp, dst_fn(n), SB16[i][:], SB16r[i], reads=[SB16r[i]])
                    stg[0] += 1
            return ep
        gemm_fm(h, KC, lambda n: w_in_fm[l][16 + n], 8, ep_store(lambda n: QMT[n]))
        gemm_fm(h, KC, lambda n: w_in_fm[l][24 + n], 48, ep_store(lambda n: GATES[n], AF.Sigmoid))
        P.barrier()
        LAT = BIG[:, 32768:32768 + 16384].rearrange("p (k t) -> p k t", k=8)
        KPE = BIG[0:64, 32768 + 16384:32768 + 16384 + 2048]
        r_lat = P.res()

        def ep_lat(n, t, b, bank):
            evac_copy(LAT[:, n, t * 512:(t + 1) * 512], bank, PSr[b], r_lat)
        gemm_fm(h, KC, lambda n: w_in_fm[l][8 + n], 8, ep_lat)

        def ep_kpe(n, t, b, bank):
            evac_copy(KPE[:, t * 512:(t + 1) * 512], bank, PSr[b], r_lat)
        gemm_fm(h, KC, lambda n: w_in_kpe[l], 1, ep_kpe, ncols=64)
        for which in range(2):
            for k4 in range(4):
                kk = which * 4 + k4
                i = k4 % 2
                P.op(dve, lambda: nc.vector.tensor_tensor(out=SB16[i][:], in0=LAT[:, kk, :], in1=LAT[:, kk, :], op=ALU.mult), reads=[r_lat], writes=[SB16r[i]])
                for t in range(4):
                    mm_group(PS[4 + t][:], PSr[4 + t], [(ONES[:], SB16[i][:, t * 512:(t + 1) * 512])], [SB16r[i]], start=(k4 == 0), stop=(k4 == 3))
            rstd_from_banks([4, 5, 6, 7], 512, lambda t: RSTD[:, t * 512:(t + 1) * 512])
            for k4 in range(4):
                kk = which * 4 + k4
                i = 2
                P.op(dve, lambda: nc.vector.scalar_tensor_tensor(out=SB16[i][:], in0=LAT[:, kk, :], scalar=par(l, P_QN if which == 0 else P_KVN, k4), in1=RSTD[:], op0=ALU.mult, op1=ALU.mult), reads=[r_lat, RSTDr], writes=[SB16r[i]])
                dst = CQN[k4] if which == 0 else SND[k4 * 128:(k4 + 1) * 128, :]
                P.dma(sp, dst, SB16[i][:], SB16r[i], reads=[SB16r[i]])
        rope_kpe_like(KPE, r_lat, 0 if seg == 0 else 1, lambda t: SND[512:576, t * 512:(t + 1) * 512], 64)
        P.barrier()

    ROPE_C = sb("ROPE_C", [64, T], BF16)
    ROPE_S = sb("ROPE_S", [64, T], BF16)
    r_rope = P.res(True, perm=True)
    r_sw = P.res(True, perm=True)
    rope_loaded = [None]

    def load_rope(st):
        if rope_loaded[0] == st:
            return
        rope_loaded[0] = st
        P.dma(pool, ROPE_C[:], rope_in[st][0], r_rope, writes=[r_rope])
        P.dma(pool, ROPE_S[:], rope_in[st][1], r_rope, writes=[r_rope])

    def rope_tile(src_ap, src_res, t, out_ap, out_res, bank_i):
        mm_group(PS[bank_i][0:64, :], PSr[bank_i], [(RT[:], src_ap)], [src_res])
        ta = TB[0:64, 0:512]
        tb = TB[0:64, 512:1024]
        P.op(dve, lambda: nc.vector.tensor_tensor(out=ta, in0=PS[bank_i][0:64, :], in1=ROPE_S[:, t * 512:(t + 1) * 512], op=ALU.mult), reads=[PSr[bank_i], r_rope], writes=[TBr])
        P.op(dve, lambda: nc.vector.tensor_tensor(out=tb, in0=src_ap, in1=ROPE_C[:, t * 512:(t + 1) * 512], op=ALU.mult), reads=[src_res, r_rope, TBr], writes=[TBr])
        P.op(dve, lambda: nc.vector.tensor_tensor(out=out_ap, in0=ta, in1=tb, op=ALU.add), reads=[TBr], writes=[out_res])

    def rope_kpe_like(src, src_res, st, dst_fn, npart):
        load_rope(st)
        for t in range(4):
            i = t % 2
            o = SB16[i][0:64, 0:512]
            rope_tile(src[:, t * 512:(t + 1) * 512], src_res, t, o, SB16r[i], 4 + t % 2)
            P.dma(sp, dst_fn(t), o, SB16r[i], reads=[SB16r[i]])

    def allgather(src, dst):
        P.barrier()
        ins = nc.gpsimd.collective_compute("AllGather", ALU.bypass, replica_groups=[list(range(NCORES))], ins=[src], outs=[dst])
        cccnt[0] += 1
        ins.then_inc(ccsem, 1)
        for e in P.engs:
            e.h.wait_ge(ccsem, cccnt[0])
        P.barrier()

    def op_fourier(seg):
        YF = A1
        r_yf = P.res()
        prompt = seg == 0
        TG = 32 if prompt else 4
        dft = dft_p if prompt else dft_s
        ubuf = [BIG[:, 32768 + i * 2048:32768 + (i + 1) * 2048].rearrange("p (a c) -> p a c", a=4) for i in range(3)]
        cbuf = [BIG[:, 32768 + 6144 + i * 2048:32768 + 6144 + (i + 1) * 2048].rearrange("p (a c) -> p a c", a=4) for i in range(3)]
        sbuf_ = [BIG[:, 32768 + 12288 + i * 2048:32768 + 12288 + (i + 1) * 2048].rearrange("p (a c) -> p a c", a=4) for i in range(3)]
        rb = [P.res(True) for _ in range(3)]

        def usrc(tg, chh):
            if prompt:
                r, j0 = tg // 4, (tg % 4) * 4
                base = RCV[r * SND_ROWS + 576:(r + 1) * SND_ROWS, :]
            else:
                j0 = tg * 4
                base = SND[576:1600, :]
            u = base.rearrange("r (h c) -> (r h) c", h=2)
            return u[j0 * 128:(j0 + 4) * 128, chh * 512:(chh + 1) * 512].rearrange("(a p) c -> p a c", p=128)

        def load(ks, chh, tg, i):
            P.dma(sp, ubuf[i], usrc(tg, chh), rb[i], writes=[rb[i]])
            P.dma(sp, cbuf[i], dft[0][ks][tg], rb[i], writes=[rb[i]])
            P.dma(sp, sbuf_[i], dft[1][ks][tg], rb[i], writes=[rb[i]])
        steps = [(ks, chh, tg) for ks in range(4) for chh in range(2) for tg in range(TG)]
        for i in range(min(2, len(steps))):
            load(*steps[i], i % 3)
        for si, (ks, chh, tg) in enumerate(steps):
            if si + 2 < len(steps):
                load(*steps[si + 2], (si + 2) % 3)
            i = si % 3
            first, last = tg == 0, tg == TG - 1
            P.acq(pe, [rb[i]], PSr if first else [])
            ins = None
            for c in range(4):
                for a in range(4):
                    nc.tensor.matmul(PS[c][:], lhsT=ubuf[i][:, a, c * 128:(c + 1) * 128], rhs=cbuf[i][:, a, :], start=(first and a == 0), stop=(last and a == 3))
                    ins = nc.tensor.matmul(PS[4 + c][:], lhsT=ubuf[i][:, a, c * 128:(c + 1) * 128], rhs=sbuf_[i][:, a, :], start=(first and a == 0), stop=(last and a == 3))
            tok = P.mark(pe, ins)
            P.rel(tok, [rb[i]], PSr if last else [])
            if last:
                for c in range(4):
                    evac_copy(YF[:, chh * 4 + c, ks * 512:(ks + 1) * 512], PS[c][:], PSr[c], r_yf)
                    evac_copy(YF[:, 8 + chh * 4 + c, ks * 512:(ks + 1) * 512], PS[4 + c][:], PSr[4 + c], r_yf)
        P.barrier()

    def op_memattn(l, seg):
        QM = A2[:, 0:8, :]
        OM = A2[:, 8:16, :]
        r_qm = P.res(True)
        r_om = P.res()
        P.dma(sp, QM, QMT.rearrange("k p t -> p k t"), r_qm, writes=[r_qm])
        MEMN = TB[:].bitcast(BF16).rearrange("p (k t) -> p k t", k=KC)
        KMEM = SB16[1][:].rearrange("p (k t) -> p k t", k=8)
        VMEM = SB16[2][:].rearrange("p (j c) -> p j c", j=2)
        dstr = transpose_in(mems[seg], 256, MEMN, None, [SF[0][:], SF[1][:]], 128, [(SB16[0], SB16r[0])], dstr=TBr)
        apply_norm(MEMN, 256, l, P_MEMN, dstr)
        r_kv = P.res()

        def ep_k(n, t, b, bank):
            evac_copy(KMEM[:, n, :], bank, PSr[b], r_kv)
        gemm_fm(MEMN, KC, lambda n: w_memk_t[l][n], 8, ep_k, ntok=256, a_reads=[dstr])
        tiles = {}
        for n in range(2):
            tiles[n] = wload(w_memv_t[l][n], KC, 128)
        for n in range(8):
            if n + 2 < 8:
                tiles[n + 2] = wload(w_memv_t[l][n + 2], KC, 128)
            wv, wr = tiles.pop(n)
            b = n % 4
            P.acq(pe, [wr, dstr], [PSr[b]])
            ins = None
            for j in range(2):
                for k in range(KC):
                    ins = nc.tensor.matmul(PS[b][:, j * 128:(j + 1) * 128], lhsT=MEMN[:, k, j * 128:(j + 1) * 128], rhs=wv[:, k, :], start=(k == 0), stop=(k == KC - 1))
            tok = P.mark(pe, ins)
            P.rel(tok, [wr, dstr], [PSr[b]])
            evac_copy(VMEM[:, :, n * 128:(n + 1) * 128], PS[b][:, 0:256].rearrange("p (j c) -> p j c", j=2), PSr[b], r_kv)
        scale = 256 ** -0.5
        pt = [SB16[0][:, 0:512], SB16[0][:, 512:1024], SB16[0][:, 1024:1536], SB16[0][:, 1536:2048]]
        ptr = [P.res() for _ in range(4)]
        g = 0
        for hh in range(4):
            for t in range(4):
                q = [QM[:, 2 * hh + dch, t * 512:(t + 1) * 512] for dch in range(2)]
                pis = []
                for jc in range(2):
                    b = 4 + g % 2
                    pi = g % 4
                    g += 1
                    mm_group(PS[b][:], PSr[b], [(KMEM[:, 2 * hh + dch, jc * 128:(jc + 1) * 128], q[dch]) for dch in range(2)], [r_kv, r_qm])
                    P.op(act, lambda: nc.scalar.activation(out=pt[pi], in_=PS[b][:], func=AF.Exp, scale=scale), reads=[PSr[b]], writes=[ptr[pi]])
                    pis.append(pi)
                mm_group(PS[6][:], PSr[6], [(ONES[:], pt[pi]) for pi in pis], [ptr[pi] for pi in pis])
                rec = RSTD[:, 0:512]
                P.op(dve, lambda: nc.vector.reciprocal(out=rec, in_=PS[6][:]), reads=[PSr[6]], writes=[RSTDr])
                for dv in range(2):
                    b = dv
                    mm_group(PS[b][:], PSr[b], [(VMEM[:, jc, (2 * hh + dv) * 128:(2 * hh + dv + 1) * 128], pt[pis[jc]]) for jc in range(2)], [r_kv] + [ptr[pi] for pi in pis])
                    P.op(dve, lambda: nc.vector.tensor_tensor(out=OM[:, 2 * hh + dv, t * 512:(t + 1) * 512], in0=PS[b][:], in1=rec, op=ALU.mult), reads=[PSr[b], RSTDr], writes=[r_om])
        P.barrier()


    def op_merge_a(l):
        YF = A1
        OM = A2[:, 8:16, :]
        gbuf = [BIG[:, 32768 + i * 2048:32768 + (i + 1) * 2048] for i in range(4)]
        gr = [P.res(True) for _ in range(4)]

        def loadg(m):
            i0 = (m % 2) * 2
            P.dma(sp, gbuf[i0], GATES[m], gr[i0], writes=[gr[i0]])
            P.dma(sp, gbuf[i0 + 1], GATES[32 + m], gr[i0 + 1], writes=[gr[i0 + 1]])
        tiles = {}
        tiles[0] = (wload(WCS[l][0], KC, 128, cast=False), wload(w_mo_t[l][0], 8, 128))
        loadg(0)
        for m in range(16):
            if m + 1 < 16:
                tiles[m + 1] = (wload(WCS[l][m + 1], KC, 128, cast=False), wload(w_mo_t[l][m + 1], 8, 128))
                loadg(m + 1)
            (wc, wcr), (wm, wmr) = tiles.pop(m)
            i0 = (m % 2) * 2
            sf = SF[m % 2]
            for t in range(4):
                bf, bm = t % 2, 2 + t % 2
                mm_group(PS[bf][:], PSr[bf], [(wc[:, k, :], YF[:, k, t * 512:(t + 1) * 512]) for k in range(KC)], [wcr])
                mm_group(PS[bm][:], PSr[bm], [(wm[:, k, :], OM[:, k, t * 512:(t + 1) * 512]) for k in range(8)], [wmr])
                t0 = TB[:, (t % 2) * 1024:(t % 2) * 1024 + 512]
                t1 = TB[:, (t % 2) * 1024 + 512:(t % 2) * 1024 + 1024]
                P.op(dve, lambda: nc.vector.tensor_tensor(out=t0, in0=PS[bf][:], in1=gbuf[i0][:, t * 512:(t + 1) * 512], op=ALU.mult), reads=[PSr[bf], gr[i0]], writes=[TBr])
                P.op(dve, lambda: nc.vector.tensor_tensor(out=t1, in0=PS[bm][:], in1=gbuf[i0 + 1][:, t * 512:(t + 1) * 512], op=ALU.mult), reads=[PSr[bm], gr[i0 + 1]], writes=[TBr])
                P.op(dve, lambda: nc.vector.tensor_tensor(out=sf[:, t * 512:(t + 1) * 512], in0=t0, in1=t1, op=ALU.add), reads=[TBr], writes=[SFr[m % 2]])
            P.dma(sp, PART[m], sf[:], SFr[m % 2], reads=[SFr[m % 2]])
        P.barrier()

    def op_mla(l, seg):
        prompt = seg == 0
        NR = NCORES if prompt else 1
        st = 0 if prompt else 1
        load_rope(st)
        cqn = A1[:, 0:4, :]
        r_cqn = P.res(True)
        P.dma(sp, cqn, CQN.rearrange("k p t -> p k t"), r_cqn, writes=[r_cqn])
        qraw = BIG[0:64, 8192:8192 + 2048]
        r_qraw = P.res()
        tiles = {}
        for n in range(2):
            tiles[n] = wload(w_uq_t[l][n], 4, 192)
        for hh in range(NH):
            if hh + 2 < NH:
                tiles[hh + 2] = wload(w_uq_t[l][hh + 2], 4, 192)
            wv, wr = tiles.pop(hh)
            i = hh % 2
            for t in range(4):
                b = t % 2
                mm_group(PS[b][:], PSr[b], [(wv[:, k, 0:128], cqn[:, k, t * 512:(t + 1) * 512]) for k in range(4)], [wr, r_cqn])
                evac_copy(SB16[i][:, t * 512:(t + 1) * 512], PS[b][:], PSr[b], SB16r[i])
                b2 = 2 + t % 2
                mm_group(PS[b2][0:64, :], PSr[b2], [(wv[:, k, 128:192], cqn[:, k, t * 512:(t + 1) * 512]) for k in range(4)], [wr, r_cqn])
                evac_copy(qraw[:, t * 512:(t + 1) * 512], PS[b2][0:64, :], PSr[b2], r_qraw, eng=act)
                rope_tile(qraw[:, t * 512:(t + 1) * 512], r_qraw, t, SB16[2][0:64, t * 512:(t + 1) * 512], SB16r[2], 4 + t % 2)
            P.dma(sp, QN[hh], SB16[i][:], SB16r[i], reads=[SB16r[i]])
            P.dma(sp, QR[hh], SB16[2][0:64, :], SB16r[2], reads=[SB16r[2]])
        P.barrier()
        ckv = [A1[:, 0:4, :], A1[:, 4:8, :]]
        r_ckv = [P.res(True), P.res(True)]
        WV = A1[:, 8:12, :]
        r_wv = r_sw
        VST = BIG[:, 12 * 2048:16 * 2048].rearrange("p (h j c) -> p h j c", h=4, j=16)
        r_vst = P.res(True)
        P.dma(pool, WV, w_uv_t[l], r_wv, writes=[r_wv])

        def lat_src(r):
            base = RCV[r * SND_ROWS:r * SND_ROWS + 512, :] if prompt else SND[0:512, :]
            return base.rearrange("(k p) t -> p k t", p=128)
        P.dma(sp, ckv[0], lat_src(0), r_ckv[0], writes=[r_ckv[0]])
        for r in range(NR):
            if r + 1 < NR:
                P.dma(sp, ckv[(r + 1) % 2], lat_src(r + 1), r_ckv[(r + 1) % 2], writes=[r_ckv[(r + 1) % 2]])
            cv, cr = ckv[r % 2], r_ckv[r % 2]
            stg = [0]

            def ep_k(n, t, b, bank):
                i = stg[0] % 2
                evac_copy(SB16[i][:, t * 512:(t + 1) * 512], bank, PSr[b], SB16r[i])
                if t == 3:
                    P.dma(sp, KT[n][:, r * T:(r + 1) * T], SB16[i][:], SB16r[i], reads=[SB16r[i]])
                    stg[0] += 1
            gemm_fm(cv, 4, lambda n: w_uk_t[l][n], NH, ep_k, a_reads=[cr])
            g = 0
            for hg in range(4):
                for j in range(16):
                    b = g % 4
                    g += 1
                    mm_group(PS[b][:], PSr[b], [(cv[:, k, j * 128:(j + 1) * 128], WV[:, k, hg * 512:(hg + 1) * 512]) for k in range(4)], [cr, r_wv])
                    evac_copy(VST[:, :, j, :], PS[b][:].rearrange("p (h c) -> p h c", h=4), PSr[b], r_vst)
                for hi in range(4):
                    P.dma(sp, VD[hg * 4 + hi][:, r * 16:(r + 1) * 16, :], VST[:, hi, :, :], r_vst, reads=[r_vst])
        P.barrier()
        O = A2
        r_o = P.res()
        TKV = NR * T
        kpe = BIG[0:64, 0:TKV]
        kpe_f = BIG[:, 0:TKV]
        r_kpe = P.res(True)
        P.op(dve, lambda: nc.vector.memset(BIG[64:128, 0:TKV], 0.0), writes=[r_kpe])
        for r in range(NR):
            src = RCV[r * SND_ROWS + 512:r * SND_ROWS + 576, :] if prompt else SND[512:576, :]
            P.dma(sp, kpe[:, r * T:(r + 1) * T], src, r_kpe, writes=[r_kpe])
        base = 16384
        qn = [BIG[:, base + i * 2048:base + (i + 1) * 2048] for i in range(2)]
        qr = [BIG[0:64, base + 4096 + i * 2048:base + 4096 + (i + 1) * 2048] for i in range(2)]
        r_q = [P.res(True), P.res(True)]
        qr_f = [BIG[:, base + 4096 + i * 2048:base + 4096 + (i + 1) * 2048] for i in range(2)]
        for i in range(2):
            P.op(dve, lambda: nc.vector.memset(BIG[64:128, base + 4096 + i * 2048:base + 4096 + (i + 1) * 2048], 0.0), writes=[r_q[i]])
        kb_ = [BIG[:, base + 8192 + i * 2048:base + 8192 + (i + 1) * 2048] for i in range(2)]
        vb_ = [BIG[:, base + 12288 + i * 2048:base + 12288 + (i + 1) * 2048].rearrange("p (c d) -> p c d", c=16) for i in range(2)]
        r_kv = [P.res(True), P.res(True)]
        pt = [SB16[0][:, 0:512], SB16[0][:, 512:1024], SB16[0][:, 1024:1536]]
        ptr = [P.res() for _ in range(3)]
        acc = [TB[:, i * 512:(i + 1) * 512] for i in range(4)]
        accr = [P.res() for _ in range(4)]
        accb = SB16[1][:, 0:512]
        scale = 192 ** -0.5
        blocks = [(hh, kb) for hh in range(NH) for kb in range(NR)]

        def loadq(hh):
            i = hh % 2
            P.dma(sp, qn[i], QN[hh], r_q[i], writes=[r_q[i]])
            P.dma(sp, qr[i], QR[hh], r_q[i], writes=[r_q[i]])

        def loadkv(bi):
            hh, kb = blocks[bi]
            i = bi % 2
            P.dma(sp, kb_[i], KT[hh][:, kb * T:(kb + 1) * T], r_kv[i], writes=[r_kv[i]])
            P.dma(sp, vb_[i], VD[hh][:, kb * 16:(kb + 1) * 16, :], r_kv[i], writes=[r_kv[i]])
        loadq(0)
        loadkv(0)
        tasks = []
        for bi, (hh, kb) in enumerate(blocks):
            for qt in range(4):
                for c in range(16):
                    tasks.append((bi, hh, kb, qt, c))
        NT_ = len(tasks)

        def emit_s(ti):
            bi, hh, kb, qt, c = tasks[ti]
            if qt == 0 and c == 1:
                if bi + 1 < len(blocks):
                    loadkv(bi + 1)
                if kb == 0 and hh + 1 < NH:
                    loadq(hh + 1)
            b = 4 + ti % 3
            mm_group(PS[b][:], PSr[b], [(kb_[bi % 2][:, c * 128:(c + 1) * 128], qn[hh % 2][:, qt * 512:(qt + 1) * 512]),
                                       (kpe_f[:, kb * T + c * 128:kb * T + (c + 1) * 128], qr_f[hh % 2][:, qt * 512:(qt + 1) * 512])],
                     [r_kv[bi % 2], r_q[hh % 2], r_kpe])
            pi = ti % 3
            P.op(act, lambda: nc.scalar.activation(out=pt[pi], in_=PS[b][:], func=AF.Exp, scale=scale), reads=[PSr[b]], writes=[ptr[pi]])

        emit_s(0)
        for ti, (bi, hh, kb, qt, c) in enumerate(tasks):
            if ti + 1 < NT_:
                emit_s(ti + 1)
            pi = ti % 3
            first = kb == 0 and c == 0
            last = kb == NR - 1 and c == 15
            mm_group(PS[qt][:], PSr[qt], [(vb_[bi % 2][:, c, :], pt[pi])], [r_kv[bi % 2], ptr[pi]], start=first, stop=last)
            if first:
                P.op(dve, lambda: nc.vector.tensor_copy(out=acc[qt], in_=pt[pi]), reads=[ptr[pi]], writes=[accr[qt]])
            else:
                P.op(dve, lambda: nc.vector.tensor_tensor(out=acc[qt], in0=acc[qt], in1=pt[pi], op=ALU.add), reads=[ptr[pi], accr[qt]], writes=[accr[qt]])
            if last:
                P.op(dve, lambda: nc.vector.tensor_copy(out=accb, in_=acc[qt]), reads=[accr[qt]], writes=[SB16r[1]])
                mm_group(PS[7][:], PSr[7], [(ONES[:], accb)], [SB16r[1]])
                rec = RSTD[:, 0:512]
                P.op(dve, lambda: nc.vector.reciprocal(out=rec, in_=PS[7][:]), reads=[PSr[7]], writes=[RSTDr])
                P.op(dve, lambda: nc.vector.tensor_tensor(out=O[:, hh, qt * 512:(qt + 1) * 512], in0=PS[qt][:], in1=rec, op=ALU.mult), reads=[PSr[qt], RSTDr], writes=[r_o])
        P.barrier()

    def op_merge_b(l):
        O = A2
        MG = A1
        r_mg = P.res()
        pbuf = SF
        gb = [SB16[0], SB16[1]]

        def loadm(m):
            i = m % 2
            P.dma(sp, gb[i][:], GATES[16 + m], SB16r[i], writes=[SB16r[i]])
            P.dma(sp, pbuf[i][:], PART[m], SFr[i], writes=[SFr[i]])
        tiles = {0: wload(w_ao_t[l][0], KC, 128)}
        loadm(0)
        for m in range(16):
            if m + 1 < 16:
                tiles[m + 1] = wload(w_ao_t[l][m + 1], KC, 128)
                loadm(m + 1)
            wv, wr = tiles.pop(m)
            i = m % 2
            for t in range(4):
                b = t % 4
                mm_group(PS[b][:], PSr[b], [(wv[:, k, :], O[:, k, t * 512:(t + 1) * 512]) for k in range(KC)], [wr])
                t0 = TB[:, (t % 2) * 512:(t % 2) * 512 + 512]
                P.op(dve, lambda: nc.vector.tensor_tensor(out=t0, in0=PS[b][:], in1=gb[i][:, t * 512:(t + 1) * 512], op=ALU.mult), reads=[PSr[b], SB16r[i]], writes=[TBr])
                P.op(dve, lambda: nc.vector.tensor_tensor(out=MG[:, m, t * 512:(t + 1) * 512], in0=t0, in1=pbuf[i][:, t * 512:(t + 1) * 512], op=ALU.add), reads=[TBr, SFr[i]], writes=[r_mg])
        P.barrier()

    def op_outproj(A, wtiles, part_in=None, part_out=None, a_reads=()):
        def loadm(m):
            wl = [(wload(ap, kc, 128), kc, ko) for (ap, kc, ko) in wtiles(m)]
            if part_in:
                P.dma(sp, SF[m % 2][:], PART[m], SFr[m % 2], writes=[SFr[m % 2]])
            return wl
        tl = {0: loadm(0)}
        for m in range(16):
            if m + 1 < 16:
                tl[m + 1] = loadm(m + 1)
            wl = tl.pop(m)
            i = m % 2
            for t in range(4):
                b = t
                pairs = []
                reads = list(a_reads)
                for (wv, wr), kc, ko in wl:
                    reads.append(wr)
                    pairs += [(wv[:, k, :], A[:, ko + k, t * 512:(t + 1) * 512]) for k in range(kc)]
                mm_group(PS[b][:], PSr[b], pairs, reads)
                if part_out:
                    evac_copy(SF[i][:, t * 512:(t + 1) * 512], PS[b][:], PSr[b], SFr[i])
                    continue
                src = PS[b][:]
                srcr = PSr[b]
                if part_in:
                    tmp = TB[:, (t % 2) * 512:(t % 2) * 512 + 512]
                    P.op(dve, lambda: nc.vector.tensor_tensor(out=tmp, in0=PS[b][:], in1=SF[i][:, t * 512:(t + 1) * 512], op=ALU.add), reads=[PSr[b], SFr[i]], writes=[TBr])
                    src, srcr = tmp, TBr
                ys = SB16[i][:, t * 512:(t + 1) * 512]
                P.op(act, lambda: nc.scalar.activation(out=ys, in_=src, func=AF.Copy), reads=[srcr], writes=[SB16r[i]])
                sq = SB16[2][:, (t % 2) * 512:(t % 2) * 512 + 512]
                P.op(act, lambda: nc.scalar.activation(out=sq, in_=src, func=AF.Square), reads=[srcr], writes=[SB16r[2]])
                mm_group(PS[4 + t][:], PSr[4 + t], [(ONES[:], sq)], [SB16r[2]], start=(m == 0), stop=(m == 15))
            if part_out:
                P.dma(sp, PART[m], SF[i][:], SFr[i], reads=[SFr[i]])
            else:
                P.dma(sp, YT[m], SB16[i][:], SB16r[i], reads=[SB16r[i]])
        if not part_out:
            rstd_from_banks([4, 5, 6, 7], D, lambda t: RSTD[:, t * 512:(t + 1) * 512])
        P.barrier()

    def op_resid(l, seg, post_off, next_l, next_off):
        XB = A1
        r_xb = P.res()
        ybuf = [BIG[:, 32768 + i * 2048:32768 + (i + 1) * 2048] for i in range(2)]
        yr = [P.res(True), P.res(True)]

        def load(m):
            i = m % 2
            P.dma(sp, ybuf[i], YT[m], yr[i], writes=[yr[i]])
            P.dma(sp, SF[i][:], XT[seg][m], SFr[i], writes=[SFr[i]])
        load(0)
        for m in range(16):
            if m + 1 < 16:
                load(m + 1)
            i = m % 2
            P.op(dve, lambda: nc.vector.scalar_tensor_tensor(out=TB[:], in0=ybuf[i], scalar=par(l, post_off, m), in1=RSTD[:], op0=ALU.mult, op1=ALU.mult), reads=[yr[i], RSTDr], writes=[TBr])
            P.op(dve, lambda: nc.vector.tensor_tensor(out=SF[i][:], in0=TB[:], in1=SF[i][:], op=ALU.add), reads=[TBr, SFr[i]], writes=[SFr[i]])
            if next_l is not None:
                P.op(act, lambda: nc.scalar.activation(out=XB[:, m, :], in_=SF[i][:], func=AF.Copy), reads=[SFr[i]], writes=[r_xb])
                sq = SB16[m % 2]
                P.op(act, lambda: nc.scalar.activation(out=sq[:], in_=SF[i][:], func=AF.Square), reads=[SFr[i]], writes=[SB16r[m % 2]])
                for t in range(4):
                    mm_group(PS[4 + t][:], PSr[4 + t], [(ONES[:], sq[:, t * 512:(t + 1) * 512])], [SB16r[m % 2]], start=(m == 0), stop=(m == 15))
            P.dma(sp, XT[seg][m], SF[i][:], SFr[i], reads=[SFr[i]])
        if next_l is not None:
            rstd_from_banks([4, 5, 6, 7], D, lambda t: RSTD[:, t * 512:(t + 1) * 512])
            apply_norm(XB, T, next_l, next_off, r_xb)
        P.barrier()


    F32V = BIG[:, 32768:65536].bitcast(F32)
    FG = F32V[:, 6144:6144 + 2050]
    r_fg = P.res()
    HALO = F32V[:, 8200:8288]
    r_halo = P.res()

    def op_ffn_edges(l):
        h = A1
        hedge = SB16[2][:, 0:32].rearrange("p (k e) -> p k e", k=KC)
        P.op(dve, lambda: nc.vector.tensor_copy(out=hedge[:, :, 0:1], in_=h[:, :, 0:1]), writes=[SB16r[2]])
        P.op(dve, lambda: nc.vector.tensor_copy(out=hedge[:, :, 1:2], in_=h[:, :, T - 1:T]), reads=[SB16r[2]], writes=[SB16r[2]])
        tiles = {}
        for n in range(2):
            tiles[n] = wload(w_g_t[l][n], KC, 128)
        for m in range(FC):
            if m + 2 < FC:
                tiles[m + 2] = wload(w_g_t[l][m + 2], KC, 128)
            wv, wr = tiles.pop(m)
            P.acq(pe, [wr, SB16r[2]], [PSr[0]] if m == 0 else [])
            ins = None
            for k in range(KC):
                ins = nc.tensor.matmul(PS[0][:, 2 * m:2 * m + 2], lhsT=wv[:, k, :], rhs=hedge[:, k, :], start=(k == 0), stop=(k == KC - 1))
            tok = P.mark(pe, ins)
            P.rel(tok, [wr, SB16r[2]], [PSr[0]] if m == FC - 1 else [])
        P.op(act, lambda: nc.scalar.activation(out=TB[:, 0:88], in_=PS[0][:, 0:88], func=AF.Copy), reads=[PSr[0]], writes=[TBr])
        P.dma(sp, ESND, TB[:, 0:88], TBr, reads=[TBr])
        allgather(ESND, ERCV)
        EB = SF[0][:, 0:8 * 88].rearrange("p (r c) -> p r c", r=8)
        P.dma(sp, EB, ERCV.rearrange("(r p) c -> p r c", p=128), SFr[0], writes=[SFr[0]])
        HL = HALO.rearrange("p (m e) -> p m e", e=2)
        P.op(dve, lambda: nc.vector.memset(HALO, 0.0), writes=[r_halo])
        for r in range(8):
            Er = EB[:, r, :].rearrange("p (m e) -> p m e", e=2)
            P.op(dve, lambda: nc.vector.scalar_tensor_tensor(out=HL[:, :, 0:1], in0=Er[:, :, 1:2], scalar=SEL[:, r:r + 1], in1=HL[:, :, 0:1], op0=ALU.mult, op1=ALU.add), reads=[SFr[0], r_halo], writes=[r_halo])
            P.op(dve, lambda: nc.vector.scalar_tensor_tensor(out=HL[:, :, 1:2], in0=Er[:, :, 0:1], scalar=SEL[:, 8 + r:9 + r], in1=HL[:, :, 1:2], op0=ALU.mult, op1=ALU.add), reads=[SFr[0], r_halo], writes=[r_halo])
        P.barrier()

    def op_ffn_main(l, seg):
        h = A1
        prompt = seg == 0
        C = F32V[:, 0:2048]
        Dd = F32V[:, 2048:4096]
        E = F32V[:, 4096:6144]
        rC, rD, rE = P.res(), P.res(), P.res()
        if not prompt:
            P.op(dve, lambda: nc.vector.memset(FG[:, 0:1], 0.0), writes=[r_fg])
            P.op(dve, lambda: nc.vector.memset(FG[:, 2049:2050], 0.0), writes=[r_fg])

        def loadw(m):
            return (wload(w_g_t[l][m], KC, 128), wload(w_u_t[l][m], KC, 128))
        tiles = {0: loadw(0)}
        for m in range(FC):
            if m + 1 < FC:
                tiles[m + 1] = loadw(m + 1)
            (wg, wgr), (wu, wur) = tiles.pop(m)
            i = m % 2
            for t in range(4):
                mm_group(PS[t][:], PSr[t], [(wg[:, k, :], h[:, k, t * 512:(t + 1) * 512]) for k in range(KC)], [wgr])
                P.op(act, lambda: nc.scalar.activation(out=FG[:, 1 + t * 512:1 + (t + 1) * 512], in_=PS[t][:], func=AF.Copy), reads=[PSr[t]], writes=[r_fg])
            if prompt:
                P.op(dve, lambda: nc.vector.tensor_copy(out=FG[:, 0:1], in_=HALO[:, 2 * m:2 * m + 1]), reads=[r_halo], writes=[r_fg])
                P.op(dve, lambda: nc.vector.tensor_copy(out=FG[:, 2049:2050], in_=HALO[:, 2 * m + 1:2 * m + 2]), reads=[r_halo], writes=[r_fg])
            for t in range(4):
                mm_group(PS[4 + t][:], PSr[4 + t], [(wu[:, k, :], h[:, k, t * 512:(t + 1) * 512]) for k in range(KC)], [wur])
            P.op(dve, lambda: nc.vector.tensor_scalar(out=C, in0=FG[:, 0:2048], scalar1=par(l, P_CW0, m), scalar2=par(l, P_CB, m), op0=ALU.mult, op1=ALU.add), reads=[r_fg], writes=[rC])
            P.op(dve, lambda: nc.vector.scalar_tensor_tensor(out=C, in0=FG[:, 1:2049], scalar=par(l, P_CW1, m), in1=C, op0=ALU.mult, op1=ALU.add), reads=[r_fg, rC], writes=[rC])
            P.op(dve, lambda: nc.vector.scalar_tensor_tensor(out=C, in0=FG[:, 2:2050], scalar=par(l, P_CW2, m), in1=C, op0=ALU.mult, op1=ALU.add), reads=[r_fg, rC], writes=[rC])
            P.op(act, lambda: nc.scalar.activation(out=Dd, in_=C, func=AF.Square), reads=[rC], writes=[rD])
            P.op(dve, lambda: nc.vector.tensor_scalar(out=Dd, in0=Dd, scalar1=0.044715, scalar2=1.0, op0=ALU.mult, op1=ALU.add), reads=[rD], writes=[rD])
            P.op(dve, lambda: nc.vector.tensor_tensor(out=Dd, in0=Dd, in1=C, op=ALU.mult), reads=[rD, rC], writes=[rD])
            P.op(act, lambda: nc.scalar.activation(out=E, in_=Dd, func=AF.Sigmoid, scale=1.5957691216057308), reads=[rD], writes=[rE])
            P.op(dve, lambda: nc.vector.tensor_tensor(out=Dd, in0=E, in1=C, op=ALU.mult), reads=[rE, rC, rD], writes=[rD])
            for t in range(4):
                P.op(dve, lambda: nc.vector.tensor_tensor(out=SB16[i][:, t * 512:(t + 1) * 512], in0=Dd[:, t * 512:(t + 1) * 512], in1=PS[4 + t][:], op=ALU.mult), reads=[rD, PSr[4 + t]], writes=[SB16r[i]])
            P.dma(sp, ACTT[m], SB16[i][:], SB16r[i], reads=[SB16r[i]])
        P.barrier()

    def op_ffn_down(l):
        AH = BIG[:, 0:22 * 2048].rearrange("p (k t) -> p k t", k=22)
        for half in range(2):
            r_a = P.res(True)
            for k in range(22):
                P.dma(sp, AH[:, k, :], ACTT[half * 22 + k], r_a, writes=[r_a] if k == 0 else [])
            r_a.w = (id(r_a), r_a.sem, r_a.cnt)
            op_outproj(AH, lambda m: [(w_d_t[l][m][half * 2], 11, 0), (w_d_t[l][m][half * 2 + 1], 11, 11)],
                       part_in=(half == 1), part_out=(half == 0), a_reads=[r_a])

    def op_transpose_out(seg):
        inb = [BIG[:, 32768 + i * 16384:32768 + (i + 1) * 16384].bitcast(F32).rearrange("p (m t) -> p m t", m=KC) for i in range(2)]
        inr = [P.res(True), P.res(True)]
        src = XT[seg].rearrange("m p t -> p m t")

        def load(blk):
            P.dma(sp, inb[blk % 2], src[:, :, blk * 512:(blk + 1) * 512], inr[blk % 2], writes=[inr[blk % 2]])
        load(0)
        g = 0
        for blk in range(4):
            if blk + 1 < 4:
                load(blk + 1)
            for sub in range(4):
                oi = (blk * 4 + sub) % 2
                for mg in range(4):
                    b = g % 8
                    g += 1
                    P.acq(pe, [inr[blk % 2]], [PSr[b]])
                    ins = None
                    for mm in range(4):
                        ins = nc.tensor.transpose(PS[b][:, mm * 128:(mm + 1) * 128], inb[blk % 2][:, mg * 4 + mm, sub * 128:(sub + 1) * 128], IDENT[:])
                    tok = P.mark(pe, ins)
                    P.rel(tok, [inr[blk % 2]], [PSr[b]])
                    evac_copy(SF[oi][:, mg * 512:(mg + 1) * 512], PS[b][:], PSr[b], SFr[oi])
                r0 = blk * 512 + sub * 128
                P.dma(sp, y_out[seg][r0:r0 + 128, :], SF[oi][:], SFr[oi], reads=[SFr[oi]])
        P.barrier()

    for l in range(L):
        gen_wcs(l)
    for seg in CFG["segs"]:
        prompt = seg == 0
        inbuf = [BIG[:, 32768 + i * 16384:32768 + (i + 1) * 16384].bitcast(F32) for i in range(2)]
        dstr = transpose_in(xs[seg], T, A1, XT[seg], inbuf, 512, [(SB16[0], SB16r[0]), (SB16[1], SB16r[1])])
        apply_norm(A1, T, 0, P_PREMIX, dstr)
        P.barrier()
        for l in range(CFG["layers"]):
            op_w_in(l, seg)
            if prompt:
                allgather(SND, RCV)
            op_fourier(seg)
            op_memattn(l, seg)
            op_merge_a(l)
            op_mla(l, seg)
            op_merge_b(l)
            op_outproj(A1, lambda m: [(w_o_t[l][m], KC, 0)])
            op_resid(l, seg, P_POSTMIX, l, P_PREFFN)
            if prompt:
                op_ffn_edges(l)
            op_ffn_main(l, seg)
            op_ffn_down(l)
            last = l == CFG["layers"] - 1
            op_resid(l, seg, P_POSTFFN, None if last else l + 1, P_PREMIX)
        op_transpose_out(seg)
    P.barrier()
    es.close()
    return nc


CFG = {"segs": [0, 1, 2], "layers": 2}


def _tile_fm(W):
    Lw, K, N = W.shape
    return np.ascontiguousarray(W.reshape(Lw, K // 128, 128, N // 128, 128).transpose(0, 3, 2, 1, 4))


def _fm_vec(v):
    Lw, n = v.shape
    return v.reshape(Lw, n // 128, 128).transpose(2, 0, 1)


_CONST_CACHE = {}


def _constants():
    if _CONST_CACHE:
        return _CONST_CACHE
    c = _CONST_CACHE
    c["ident_in"] = np.eye(128, dtype=np.float32)
    c["ones_in"] = np.ones((128, 128), dtype=NPBF)
    rt = np.zeros((64, 64), np.float32)
    for m in range(32):
        rt[m + 32, m] = -1.0
    for m in range(32, 64):
        rt[m - 32, m] = 1.0
    c["rt_in"] = rt.astype(NPBF)
    p = np.arange(128)[:, None, None]
    bb = np.arange(2)[None, :, None]
    m = np.arange(256)[None, None, :]
    ang = 2 * np.pi * (((bb * 128 + p) * m) % 256) / 256.0
    cc = np.stack([np.cos(ang), np.sin(ang)], axis=1) / 16.0
    c["cc_in"] = cc.astype(NPBF)
    inv_freq = 1.0 / (10000.0 ** (np.arange(0, 64, 2, dtype=np.float32) / 64.0))
    c["inv_freq"] = inv_freq.astype(np.float32)

    def dft_tiles(S, ntg, kbase):
        tab_c = np.cos(2 * np.pi * np.arange(S) / S) / np.sqrt(S)
        tab_s = -np.sin(2 * np.pi * np.arange(S) / S) / np.sqrt(S)
        t = (np.arange(ntg)[:, None, None] * 4 + np.arange(4)[None, None, :]) * 128 + np.arange(128)[None, :, None]
        out = np.empty((2, 4, ntg, 128, 4, 512), dtype=NPBF)
        for ks in range(4):
            k = kbase + ks * 512 + np.arange(512)
            ph = (t[..., None].astype(np.int64) * k[None, None, None, :].astype(np.int64)) % S
            out[0, ks] = tab_c[ph].astype(NPBF)
            out[1, ks] = tab_s[ph].astype(NPBF)
        return out
    c["dft_s"] = dft_tiles(2048, 4, 0)
    c["dft_p"] = [dft_tiles(16384, 32 if 0 in CFG["segs"] else 1, 2048 * cc_) for cc_ in range(NCORES)]
    return c


def _rope_table(base):
    c = _constants()
    pos = (base + np.arange(T)).astype(np.float32)
    ang = pos[:, None] * c["inv_freq"][None, :]
    cs = np.cos(ang).astype(np.float32).T
    sn = np.sin(ang).astype(np.float32).T
    return np.stack([np.concatenate([cs, cs], 0), np.concatenate([sn, sn], 0)], 0)


def _prep_shared(inp):
    w_in = inp["w_in"]
    sh = {}
    cols = np.concatenate([np.arange(0, 1024), np.arange(1024, 2048), np.arange(2112, 3136), np.arange(3136, 9280)])
    sh["w_in_fm"] = _tile_fm(w_in[:, :, cols])
    sh["w_in_kpe"] = np.ascontiguousarray(w_in[:, :, 2048:2112].reshape(L, KC, 128, 64).transpose(0, 2, 1, 3))
    sh["w_uq_t"] = np.ascontiguousarray(inp["w_uq"].reshape(L, 4, 128, NH, 192).transpose(0, 3, 2, 1, 4))
    ukv = inp["w_ukv"].reshape(L, 4, 128, NH, 256)
    sh["w_uk_t"] = np.ascontiguousarray(ukv[..., 0:128].transpose(0, 3, 2, 1, 4))
    sh["w_uv_t"] = np.ascontiguousarray(ukv[..., 128:256].transpose(0, 2, 1, 3, 4)).reshape(L, 128, 4, 2048)
    sh["w_memk_t"] = _tile_fm(inp["w_mem_kv"][:, :, 0:1024])
    sh["w_memv_t"] = _tile_fm(inp["w_mem_kv"][:, :, 1024:2048])
    sh["w_fo_t"] = np.ascontiguousarray(inp["w_fourier_out"].reshape(L, 8, 128, 2048).transpose(0, 2, 1, 3))
    sh["w_mo_t"] = _tile_fm(inp["w_mem_out"])
    sh["w_ao_t"] = _tile_fm(inp["w_attn_out"])
    sh["w_o_t"] = _tile_fm(inp["w_o"])
    sh["w_g_t"] = _tile_fm(inp["w_ffn_gate"])
    sh["w_u_t"] = _tile_fm(inp["w_ffn_up"])
    wd = _tile_fm(inp["w_ffn_down"])
    sh["w_d_t"] = np.ascontiguousarray(wd.reshape(L, 16, 128, 4, 11, 128).transpose(0, 1, 3, 2, 4, 5))
    par = np.zeros((128, L, NPAR), np.float32)
    for off, key in ((P_PREMIX, "pre_mix_norm"), (P_POSTMIX, "post_mix_norm"), (P_PREFFN, "pre_ffn_norm"), (P_POSTFFN, "post_ffn_norm"),
                     (P_QN, "q_norm"), (P_KVN, "kv_norm"), (P_MEMN, "mem_norm"), (P_CB, "ffn_conv_b")):
        v = _fm_vec(np.asarray(inp[key], np.float32))
        par[:, :, off:off + v.shape[2]] = v
    cw = np.asarray(inp["ffn_conv_w"], np.float32)
    for j, off in enumerate((P_CW0, P_CW1, P_CW2)):
        par[:, :, off:off + FC] = _fm_vec(np.ascontiguousarray(cw[:, j, :]))
    sh["par_in"] = np.ascontiguousarray(par.reshape(128, L * NPAR))
    c = _constants()
    for k in ("ident_in", "ones_in", "rt_in", "cc_in", "dft_s"):
        sh[k] = c[k]
    return sh


def make_in_maps(inp, cores=range(NCORES)):
    inp = {k: np.asarray(v) for k, v in inp.items()}
    sh = _prep_shared(inp)
    c = _constants()
    rope_s = _rope_table(0)
    in_maps = []
    for ci in cores:
        d = dict(sh)
        d["xs"] = np.ascontiguousarray(np.stack([inp["x_prompt"][0, ci * T:(ci + 1) * T], inp["x_sample"][2 * ci], inp["x_sample"][2 * ci + 1]], 0))
        d["mems"] = np.ascontiguousarray(np.stack([inp["mem_prompt"][0], inp["mem_sample"][2 * ci], inp["mem_sample"][2 * ci + 1]], 0))
        d["rope_in"] = np.ascontiguousarray(np.stack([_rope_table(ci * T), rope_s], 0)).astype(np.float32)
        sel = np.zeros((128, 16), np.float32)
        if ci - 1 >= 0:
            sel[:, ci - 1] = 1.0
        if ci + 1 < NCORES:
            sel[:, 8 + ci + 1] = 1.0
        d["sel_in"] = sel
        d["dft_p"] = c["dft_p"][ci] if 0 in CFG["segs"] else c["dft_p"][ci][:, :, 0:1]
        in_maps.append(d)
    return in_maps


def kernel(**inp):
    in_maps = make_in_maps(inp)
    nc = build_program()
    res = run_bass_kernel_spmd(nc, in_maps, core_ids=list(range(NCORES)))
    y_prompt = np.empty((1, NCORES * T, D), np.float32)
    y_sample = np.empty((2 * NCORES, T, D), np.float32)
    for ci in range(NCORES):
        yo = res.results[ci]["y_out"]
        y_prompt[0, ci * T:(ci + 1) * T] = yo[0]
        y_sample[2 * ci] = yo[1]
        y_sample[2 * ci + 1] = yo[2]
    kernel.last_results = res.results
    return (y_prompt, y_sample)
```
